# Optimizing a Trainium2 kernel written in Bass

```python
import jax, jax.numpy as jnp
from jax import lax
import numpy as np

D_MODEL = 2048
BATCH = 4
SEQ = 4096
DEPTH = 2

MLSTM_HEADS = 4
MLSTM_QK_DIM = 256
MLSTM_V_DIM = 512
MLSTM_QK_WIDTH = MLSTM_HEADS * MLSTM_QK_DIM
MLSTM_WIDTH = MLSTM_HEADS * MLSTM_V_DIM
CHUNK = 64
N_GATE_COLS = 4 * MLSTM_HEADS
LRU_WIDTH = D_MODEL
LRU_BLOCKS = 16
LRU_BLOCK_DIM = LRU_WIDTH // LRU_BLOCKS
LRU_C = 8.0
CONV_W = 4
CONV_LEFT = 2
N_DIR = 2
NORM_EPS = 1e-6
N_IN = 2 * MLSTM_QK_WIDTH + 3 * MLSTM_WIDTH + N_GATE_COLS + 2 * LRU_WIDTH + 2 * D_MODEL

kernel_name = "hybrid_mlstm_rglru_gated_parallel_encoder"


def _rms_norm(x, g):
    xf = x.astype(jnp.float32)
    y = xf * lax.rsqrt(jnp.mean(xf * xf, axis=-1, keepdims=True) + NORM_EPS)
    return (y * g.astype(jnp.float32)).astype(x.dtype)


def _to_heads(t, head_dim):
    b, s, _ = t.shape
    return t.reshape(b, s, -1, head_dim).transpose(0, 2, 1, 3).astype(jnp.float32)


def _mlstm_one_direction(q, k, v, i_pre, f_pre):
    bsz, nh, s, _ = q.shape
    nc = s // CHUNK

    def chunks(t):
        return jnp.moveaxis(t.reshape((bsz, nh, nc, CHUNK) + t.shape[3:]), 2, 0)

    b_cum = jnp.cumsum(chunks(jax.nn.log_sigmoid(f_pre)), axis=-1)
    lower = jnp.tril(jnp.ones((CHUNK, CHUNK), dtype=bool))

    def step(carry, xs):
        c_st, n_st, m_st = carry
        qc, kc, vc, ic, bc = xs
        d_log = jnp.where(lower, bc[..., :, None] - bc[..., None, :] + ic[..., None, :], -jnp.inf)
        inter_log = bc + m_st[..., None]
        m_row = jnp.maximum(inter_log, jnp.max(d_log, axis=-1))
        w_intra = jnp.exp(d_log - m_row[..., None])
        w_inter = jnp.exp(inter_log - m_row)
        scores = jnp.einsum('bhjk,bhsk->bhjs', qc, kc) * w_intra
        num = (jnp.einsum('bhjs,bhsv->bhjv', scores, vc)
               + w_inter[..., None] * jnp.einsum('bhjk,bhvk->bhjv', qc, c_st))
        den = jnp.sum(scores, axis=-1) + w_inter * jnp.einsum('bhjk,bhk->bhj', qc, n_st)
        h = num / jnp.maximum(jnp.abs(den), jnp.exp(-m_row))[..., None]
        g_tot = bc[..., -1]
        w_log = g_tot[..., None] - bc + ic
        m_new = jnp.maximum(g_tot + m_st, jnp.max(w_log, axis=-1))
        w_k = jnp.exp(w_log - m_new[..., None])
        decay = jnp.exp(g_tot + m_st - m_new)
        c_new = decay[..., None, None] * c_st + jnp.einsum('bhsv,bhsk->bhvk', vc * w_k[..., None], kc)
        n_new = decay[..., None] * n_st + jnp.einsum('bhs,bhsk->bhk', w_k, kc)
        return (c_new, n_new, m_new), h

    init = (jnp.zeros((bsz, nh, MLSTM_V_DIM, MLSTM_QK_DIM), jnp.float32),
            jnp.zeros((bsz, nh, MLSTM_QK_DIM), jnp.float32),
            jnp.zeros((bsz, nh), jnp.float32))
    _, h = lax.scan(step, init, (chunks(q), chunks(k), chunks(v), chunks(i_pre), b_cum))
    return jnp.moveaxis(h, 0, 2).reshape(bsz, nh, s, MLSTM_V_DIM)


def _mlstm_branch(q_p, k_p, v_p, o_p, z_p, gif_p, b_if, head_g):
    bsz, s, _ = v_p.shape
    q = _to_heads(q_p, MLSTM_QK_DIM)
    k = _to_heads(k_p, MLSTM_QK_DIM) * (MLSTM_QK_DIM ** -0.5)
    v = _to_heads(v_p, MLSTM_V_DIM)
    gates = (gif_p.astype(jnp.float32) + b_if.astype(jnp.float32)).transpose(0, 2, 1)
    i_f, i_b, f_f, f_b = jnp.split(gates, 4, axis=1)
    flip = lambda t: jnp.flip(t, axis=2)
    h_fwd = _mlstm_one_direction(q, k, v, i_f, f_f)
    h_bwd = flip(_mlstm_one_direction(flip(q), flip(k), flip(v), flip(i_b), flip(f_b)))
    h = h_fwd + h_bwd
    h = h * lax.rsqrt(jnp.mean(h * h, axis=-1, keepdims=True) + NORM_EPS)
    h = h.transpose(0, 2, 1, 3).reshape(bsz, s, MLSTM_WIDTH) * head_g.astype(jnp.float32)
    h = h * jax.nn.sigmoid(o_p.astype(jnp.float32)) * jax.nn.silu(z_p.astype(jnp.float32))
    return h.astype(v_p.dtype)


def _rglru_one_direction(xc, w_r, b_r, w_i, b_i, lam, reverse):
    bsz, s, w = xc.shape
    xb = xc.reshape(bsz, s, LRU_BLOCKS, LRU_BLOCK_DIM)
    r = jax.nn.sigmoid(jnp.einsum('bsnc,ncd->bsnd', xb, w_r.astype(jnp.float32)).reshape(bsz, s, w)
                       + b_r.astype(jnp.float32))
    i = jax.nn.sigmoid(jnp.einsum('bsnc,ncd->bsnd', xb, w_i.astype(jnp.float32)).reshape(bsz, s, w)
                       + b_i.astype(jnp.float32))
    log_a = -LRU_C * r * jax.nn.softplus(-lam.astype(jnp.float32))
    a = jnp.exp(log_a)
    u = jnp.sqrt(-jnp.expm1(2.0 * log_a)) * (i * xc)

    def combine(left, right):
        a1, b1 = left
        a2, b2 = right
        return a1 * a2, a2 * b1 + b2

    _, h = lax.associative_scan(combine, (a, u), reverse=reverse, axis=1)
    return h


def _rglru_branch(x_p, z_p, conv_w, conv_b, w_rg, b_rg, lam):
    s = x_p.shape[1]
    xf = x_p.astype(jnp.float32)
    xpad = jnp.pad(xf, ((0, 0), (CONV_LEFT, CONV_W - 1 - CONV_LEFT), (0, 0)))
    cw = conv_w.astype(jnp.float32)
    xc = conv_b.astype(jnp.float32) + sum(xpad[:, t:t + s] * cw[t] for t in range(CONV_W))
    h = (_rglru_one_direction(xc, w_rg[0, 0], b_rg[0, 0], w_rg[0, 1], b_rg[0, 1], lam[0], False)
         + _rglru_one_direction(xc, w_rg[1, 0], b_rg[1, 0], w_rg[1, 1], b_rg[1, 1], lam[1], True))
    return (h * jax.nn.silu(z_p.astype(jnp.float32))).astype(x_p.dtype)


def _hybrid_layer(x, norm_g, w_in, b_if, head_g, conv_w, conv_b, w_rg, b_rg, lam,
                  w_branch_a, w_branch_b, w_out):
    h = _rms_norm(x, norm_g)
    proj = jnp.einsum('bsd,dn->bsn', h, w_in.astype(h.dtype))
    sizes = [MLSTM_QK_WIDTH, MLSTM_QK_WIDTH, MLSTM_WIDTH, MLSTM_WIDTH, MLSTM_WIDTH,
             N_GATE_COLS, LRU_WIDTH, LRU_WIDTH, D_MODEL]
    cuts = [int(c) for c in np.cumsum(sizes)]
    q_p, k_p, v_p, o_p, za_p, gif_p, xb_p, zb_p, ga_p, gb_p = jnp.split(proj, cuts, axis=-1)
    ya = _mlstm_branch(q_p, k_p, v_p, o_p, za_p, gif_p, b_if, head_g)
    yb = _rglru_branch(xb_p, zb_p, conv_w, conv_b, w_rg, b_rg, lam)
    ya = jnp.einsum('bsw,wd->bsd', ya, w_branch_a.astype(ya.dtype))
    yb = jnp.einsum('bsw,wd->bsd', yb, w_branch_b.astype(yb.dtype))
    merged = (jax.nn.sigmoid(ga_p.astype(jnp.float32)) * ya.astype(jnp.float32)
              + jax.nn.sigmoid(gb_p.astype(jnp.float32)) * yb.astype(jnp.float32)).astype(x.dtype)
    return x + jnp.einsum('bsd,de->bse', merged, w_out.astype(x.dtype))


def setup_inputs(seed: int = 0) -> dict:
    key = jax.random.key(seed)
    ks = jax.random.split(key, 16)
    f32 = jnp.float32
    nrm = lambda k, shape, scale: jax.random.normal(k, shape, f32) * scale
    x = jax.random.normal(ks[0], (BATCH, SEQ, D_MODEL), f32)
    norm_g = 1.0 + nrm(ks[1], (DEPTH, D_MODEL), 0.02)
    w_in = nrm(ks[2], (DEPTH, D_MODEL, N_IN), D_MODEL ** -0.5)
    i_bias = nrm(ks[3], (DEPTH, 2 * MLSTM_HEADS), 0.1)
    f_bias = 3.0 + 3.0 * jax.random.uniform(ks[4], (DEPTH, 2 * MLSTM_HEADS), f32)
    b_if = jnp.concatenate([i_bias, f_bias], axis=-1)
    head_g = 1.0 + nrm(ks[5], (DEPTH, MLSTM_WIDTH), 0.02)
    conv_w = nrm(ks[6], (DEPTH, CONV_W, LRU_WIDTH), CONV_W ** -0.5)
    conv_b = nrm(ks[7], (DEPTH, LRU_WIDTH), 0.02)
    w_rg = nrm(ks[8], (DEPTH, N_DIR, 2, LRU_BLOCKS, LRU_BLOCK_DIM, LRU_BLOCK_DIM), LRU_BLOCK_DIM ** -0.5)
    b_rg = nrm(ks[9], (DEPTH, N_DIR, 2, LRU_WIDTH), 0.02)
    u = jax.random.uniform(ks[10], (DEPTH, N_DIR, LRU_WIDTH), f32, 0.9, 0.999)
    lru_lambda = jnp.log(u) - jnp.log1p(-u)
    w_branch_a = nrm(ks[11], (DEPTH, MLSTM_WIDTH, D_MODEL), MLSTM_WIDTH ** -0.5)
    w_branch_b = nrm(ks[12], (DEPTH, LRU_WIDTH, D_MODEL), LRU_WIDTH ** -0.5)
    w_out = nrm(ks[13], (DEPTH, D_MODEL, D_MODEL), D_MODEL ** -0.5)
    final_g = 1.0 + nrm(ks[14], (D_MODEL,), 0.02)
    return {"x": x, "norm_g": norm_g, "w_in": w_in, "b_if": b_if, "head_g": head_g,
            "conv_w": conv_w, "conv_b": conv_b, "w_rg": w_rg, "b_rg": b_rg,
            "lru_lambda": lru_lambda, "w_branch_a": w_branch_a, "w_branch_b": w_branch_b,
            "w_out": w_out, "final_g": final_g}


def reference(x, norm_g, w_in, b_if, head_g, conv_w, conv_b, w_rg, b_rg, lru_lambda,
              w_branch_a, w_branch_b, w_out, final_g):
    h = x
    for layer in range(DEPTH):
        h = _hybrid_layer(h, norm_g[layer], w_in[layer], b_if[layer], head_g[layer],
                          conv_w[layer], conv_b[layer], w_rg[layer], b_rg[layer],
                          lru_lambda[layer], w_branch_a[layer], w_branch_b[layer], w_out[layer])
    return _rms_norm(h, final_g)
```

```python
import numpy as np
import concourse.bass as bass
import concourse.mybir as mybir
from concourse.bass_utils import run_bass_kernel_spmd

F32 = mybir.dt.float32
BF16 = mybir.dt.bfloat16
U8 = mybir.dt.uint8
AF = mybir.ActivationFunctionType
ALU = mybir.AluOpType
AX = mybir.AxisListType

D = 2048
L = 128
EPS = 1e-6
ENG = ["pe", "act", "dve", "pool", "sp"]


def _merge(d, s):
    for k, v in s.items():
        if d.get(k, 0) < v:
            d[k] = v


class Buf:
    __slots__ = ("name", "old", "w", "r", "sem")

    def __init__(self, name):
        self.name = name
        self.old = {}
        self.w = {}
        self.r = {}
        self.sem = None


class Sched:
    def __init__(self, nc, ndma):
        self.nc = nc
        self.h = {}
        for e in ENG:
            self.h[("e", e)] = nc.alloc_semaphore("es_" + e)
        for i in range(ndma):
            self.h[("d", i)] = nc.alloc_semaphore("ds%d" % i)
        self.ecnt = {e: 0 for e in ENG}
        self.dval = [0] * ndma
        self.dfree = list(range(ndma))
        self.dbufs = []
        self.seen = {e: {} for e in ENG}
        self.streams = {e: [] for e in ENG}

    def op(self, eng, fn, reads=(), writes=(), pwrites=(), dma=None, dinc=16):
        deps = {}
        for b in reads:
            _merge(deps, b.w)
        for b in writes:
            old = {}
            _merge(old, b.w)
            _merge(old, b.r)
            b.old = old
            b.w = {}
            b.r = {}
        for b in list(writes) + list(pwrites):
            _merge(deps, b.old)
            _merge(deps, b.r)
        if dma is not None:
            if dma.sem is None:
                dma.sem = self.dfree.pop()
                self.dbufs.append(dma)
            key = ("d", dma.sem)
            self.dval[dma.sem] += dinc
            val = self.dval[dma.sem]
            inc = dinc
        else:
            key = ("e", eng)
            self.ecnt[eng] += 1
            val = self.ecnt[eng]
            inc = 1
        waits = []
        seen = self.seen[eng]
        for k, v in deps.items():
            if k == ("e", "pe") and eng == "pe":
                continue
            if seen.get(k, 0) < v:
                seen[k] = v
                waits.append((k, v))
        self.streams[eng].append((waits, fn, key, inc))
        ev = {key: val}
        for b in reads:
            _merge(b.r, ev)
        for b in list(writes) + list(pwrites):
            _merge(b.w, ev)

    def newgen(self, b):
        old = {}
        _merge(old, b.w)
        _merge(old, b.r)
        b.old = old
        b.w = {}
        b.r = {}

    def barrier(self):
        allev = {("e", e): self.ecnt[e] for e in ENG if self.ecnt[e] > 0}
        for i, v in enumerate(self.dval):
            if v > 0:
                allev[("d", i)] = v
        for e in ENG:
            waits = []
            seen = self.seen[e]
            for k, v in allev.items():
                if k == ("e", e):
                    continue
                if seen.get(k, 0) < v:
                    seen[k] = v
                    waits.append((k, v))
            if waits:
                self.streams[e].append((waits, None, None, 0))
        for b in self.dbufs:
            b.sem = None
        self.dbufs = []
        self.dfree = list(range(len(self.dval)))

    def replay(self, e, name):
        for waits, fn, key, inc in self.streams[name]:
            for k, v in waits:
                e.wait_ge(self.h[k], v)
            if fn is not None:
                fn(e).then_inc(self.h[key], inc)


class Arena:
    def __init__(self, nc, nbytes):
        self.t = nc.alloc_sbuf_tensor("arena", [128, nbytes], U8)
        self.n = nbytes
        self.cur = 0
        self.base = 0

    def alloc(self, shape, dt):
        esz = 4 if dt == F32 else 2
        nb = int(np.prod(shape[1:])) * esz
        nb_al = (nb + 63) // 64 * 64
        assert self.cur + nb_al <= self.n, ("SBUF arena overflow", self.cur, nb_al, self.n)
        v = self.t[:, self.cur:self.cur + nb].bitcast(dt)
        self.cur += nb_al
        if len(shape) == 3:
            v = v.rearrange("p (a b) -> p a b", a=shape[1])
        elif len(shape) == 4:
            v = v.rearrange("p (a b c) -> p a b c", a=shape[1], b=shape[2])
        return v[0:shape[0]]

    def mark(self):
        self.base = self.cur

    def reset(self):
        self.cur = self.base


class Cfg:
    def __init__(self, S=4096, HPC=4, NCT=16, DEPTH=2, final=True, debug=(), stop=None, split=1, groups=None):
        self.stop = stop
        self.split = split
        self.groups = groups
        self.S, self.HPC, self.NCT, self.DEPTH, self.final = S, HPC, NCT, DEPTH, final
        self.NT = S // 128
        self.NC = S // L
        self.NB = S // 512
        self.QW = HPC * 256
        self.VW = HPC * 512
        self.CW = NCT * 128
        self.NFM = 2 * self.QW + 2 * self.CW + 2 * D
        self.NTM = self.QW + 3 * self.VW
        self.debug = debug


def build_program(cfg):
    S, HPC, NCT, NT, NC, NB = cfg.S, cfg.HPC, cfg.NCT, cfg.NT, cfg.NC, cfg.NB
    QW, VW, CW, NFM, NTM = cfg.QW, cfg.VW, cfg.CW, cfg.NFM, cfg.NTM
    G4 = 4 * HPC
    nc = bass.Bass("TRN2", target_bir_lowering=False)
    DEPTH = cfg.DEPTH

    def din(name, shape, dt=F32):
        return nc.dram_tensor(name, list(shape), dt, kind="ExternalInput").ap()

    def dscr(name, shape, dt):
        kind = "ExternalOutput" if name in cfg.debug else "Internal"
        return nc.dram_tensor(name, list(shape), dt, kind=kind).ap()

    x_in = din("x", [S, D])
    wfm = din("wfm", [DEPTH, D, NFM])
    wtm = din("wtm", [DEPTH, D, NTM])
    wg = din("wg", [DEPTH, 128, 16 * G4])
    bg = din("bg", [DEPTH, G4, 1])
    normg = din("normg", [DEPTH, D])
    headg = din("headg", [DEPTH, VW])
    cvec = din("cvec", [DEPTH, CW, 12])
    wrg = din("wrg", [DEPTH, 4, NCT, 128, 128])
    wa = din("wa", [DEPTH, VW, D])
    wb = din("wb", [DEPTH, CW, D])
    wout = din("wout", [DEPTH, D, D])
    finalg = din("finalg", [D])
    c_ident = din("c_ident", [128, 128])
    c_mask = din("c_mask", [L, 2, L])
    c_sel = din("c_sel", [HPC, HPC, 128])
    c_flag = din("c_flag", [128, 1])
    out = nc.dram_tensor("out", [S, D], F32, kind="ExternalOutput").ap()

    QT = dscr("QT", [QW, S], BF16)
    KT = dscr("KT", [QW, S], BF16)
    KK = dscr("KK", [S, QW], BF16)
    VE = dscr("VE", [2, S, VW], BF16)
    SO = dscr("SO", [S, VW], BF16)
    SZ = dscr("SZ", [S, VW], BF16)
    XB = dscr("XB", [CW, S], BF16)
    ZB = dscr("ZB", [CW, S], BF16)
    GA = dscr("GA", [D, S], BF16)
    GB = dscr("GB", [D, S], BF16)
    HF = dscr("HF", [S, VW], F32)
    YAT = dscr("YAT", [VW, S], BF16)
    YBT = dscr("YBT", [CW, S], BF16)
    MT = dscr("MT", [D, S], BF16)
    X1 = dscr("X1", [S, D], F32)
    HTS = dscr("HTS", [128, 16, S], BF16)
    PP = dscr("PP", [S, D], F32)
    PSL = [dscr("PS%d" % i, [S, D], F32) for i in range(DEPTH)]
    dbuf = {n: Buf(n) for n in ["QT", "KT", "KK", "VE", "SO", "SZ", "XB", "ZB", "GA", "GB", "HF",
                                "YAT", "YBT", "MT", "X1", "OUT", "IN", "PP", "HTS"] + ["PS%d" % i for i in range(DEPTH)]}

    A = Arena(nc, 206 * 1024)
    banks = [nc.alloc_psum_tensor("bank%d" % i, [128, 512], F32) for i in range(8)]
    bankb = [Buf("bank%d" % i) for i in range(8)]
    sc = Sched(nc, 84)
    IN = dbuf["IN"]

    identf = A.alloc([128, 128], F32)
    identb = A.alloc([128, 128], BF16)
    maskb = A.alloc([L, 2, L], BF16)
    self_ = A.alloc([HPC, HPC, 128], F32)
    b_const = Buf("const")
    flagt = A.alloc([128, 1], F32)
    sc.op("sp", lambda e: e.dma_start(out=identf, in_=c_ident[:, :]), reads=[IN], pwrites=[b_const], dma=b_const)
    sc.op("sp", lambda e: e.dma_start(out=self_, in_=c_sel[:, :, :]), reads=[IN], pwrites=[b_const], dma=b_const)
    sc.op("sp", lambda e: e.dma_start(out=flagt, in_=c_flag[:, :]), reads=[IN], pwrites=[b_const], dma=b_const)
    sc.op("pool", lambda e: e.dma_start(out=identb, in_=c_ident[:, :]), reads=[IN], pwrites=[b_const], dma=b_const)
    sc.op("pool", lambda e: e.dma_start(out=maskb, in_=c_mask[:, :, :]), reads=[IN], pwrites=[b_const], dma=b_const)
    ETK = A.alloc([128, NT, 2 * HPC], F32)
    ECH = A.alloc([L, NC, 4 * HPC], F32)
    ECHB = A.alloc([L, NC, 2 * HPC], BF16)
    DEC = A.alloc([128, 2 * HPC, NC], F32)
    b_tab = Buf("tables")
    A.mark()

    def dump(name, ap, b, shape, dt=F32):
        t = nc.dram_tensor("dbg_" + name, list(shape), dt, kind="ExternalOutput").ap()
        db = Buf("dbg_" + name)
        sc.op("sp", lambda e: e.dma_start(out=t, in_=ap), reads=[b], writes=[db], dma=db)

    def norm_tile(l, tt, xsrc, xsrc_buf, xt, b_xt, junk, b_junk, st, b_st, gbc, b_gbc, xn, b_xn, part=None):
        if part in (None, "A"):
            sc.op("sp", lambda e: e.dma_start(out=xt, in_=xsrc[tt * 128:(tt + 1) * 128, :]),
                  reads=[xsrc_buf], writes=[b_xt], dma=b_xt)
            sc.op("act", lambda e: e.activation(out=junk, in_=xt, func=AF.Square), reads=[b_xt], writes=[b_junk])
            sc.op("dve", lambda e: e.reduce_sum(out=st[:, 0:1], in_=junk, axis=AX.X), reads=[b_junk], writes=[b_st])
        if part in (None, "B"):
            sc.op("act", lambda e: e.activation(out=st[:, 1:2], in_=st[:, 0:1], func=AF.Sqrt, bias=st[:, 3:4], scale=1.0 / D),
                  reads=[b_st], pwrites=[b_st])
            sc.op("dve", lambda e: e.reciprocal(out=st[:, 2:3], in_=st[:, 1:2]), reads=[b_st], pwrites=[b_st])
            sc.op("dve", lambda e: e.scalar_tensor_tensor(out=xn, in0=xt, scalar=st[:, 2:3], in1=gbc,
                                                          op0=ALU.mult, op1=ALU.mult),
                  reads=[b_xt, b_st, b_gbc], writes=[b_xn])

    def layer(l, xsrc, xsrc_buf, last):
        A.reset()
        gbc = A.alloc([128, D], F32); b_gbc = Buf("gbc")
        sc.op("sp", lambda e: e.dma_start(out=gbc, in_=normg[l].partition_broadcast(128)),
              reads=[IN], writes=[b_gbc], dma=b_gbc)
        wgb = A.alloc([128, 16, G4], BF16); b_wgb = Buf("wgb")
        wgf = A.alloc([128, 16 * G4], F32); b_wgf = Buf("wgf")
        sc.op("sp", lambda e: e.dma_start(out=wgf, in_=wg[l]), reads=[IN], writes=[b_wgf], dma=b_wgf)
        sc.op("dve", lambda e: e.tensor_copy(out=wgb, in_=wgf.rearrange("p (a b) -> p a b", a=16)),
              reads=[b_wgf], writes=[b_wgb])
        bgt = A.alloc([G4, 1], F32); b_bgt = Buf("bgt")
        sc.op("sp", lambda e: e.dma_start(out=bgt, in_=bg[l]), reads=[IN], writes=[b_bgt], dma=b_bgt)
        xt = [A.alloc([128, D], F32) for _ in range(2)]; b_xt = [Buf("xt%d" % i) for i in range(2)]
        xn = [A.alloc([128, D], BF16) for _ in range(2)]; b_xn = [Buf("xn%d" % i) for i in range(2)]
        junk = A.alloc([128, D], BF16); b_junk = Buf("junk")
        st = [A.alloc([128, 4], F32) for _ in range(2)]; b_st = [Buf("st%d" % i) for i in range(2)]
        for i in range(2):
            sc.op("pool", lambda e, i=i: e.memset(st[i][:, 3:4], EPS), writes=[b_st[i]])
        hTs = [A.alloc([128, 16, 128], BF16) for _ in range(2)]; b_hTs = [Buf("hTs%d" % i) for i in range(2)]
        GT = A.alloc([G4, S], F32); b_GT = Buf("GT")
        sc.newgen(b_GT)
        sc.newgen(dbuf["HTS"])

        def transposes(i, b_dst_list, dst_fn):
            for half in range(2):
                pb = banks[6 + half][:].bitcast(BF16)
                for j in range(8):
                    jj = half * 8 + j
                    sc.op("pe", lambda e, jj=jj, j=j, pb=pb: e.transpose(out=pb[:, j * 128:(j + 1) * 128],
                                                                        in_=xn[i][:, jj * 128:(jj + 1) * 128],
                                                                        identity=identb),
                          reads=[b_xn[i], b_const], writes=[bankb[6 + half]] if j == 0 else (),
                          pwrites=() if j == 0 else [bankb[6 + half]])
                eng = "act" if half == 0 else "dve"
                dst = dst_fn(half)
                src = pb.rearrange("p (a b) -> p a b", a=8)
                if eng == "act":
                    sc.op("act", lambda e, dst=dst, src=src: e.activation(out=dst, in_=src, func=AF.Copy),
                          reads=[bankb[6 + half]], pwrites=b_dst_list)
                else:
                    sc.op("dve", lambda e, dst=dst, src=src: e.tensor_copy(out=dst, in_=src),
                          reads=[bankb[6 + half]], pwrites=b_dst_list)

        norm_tile(l, 0, xsrc, xsrc_buf, xt[0], b_xt[0], junk, b_junk, st[0], b_st[0], gbc, b_gbc, xn[0], b_xn[0])
        for tt in range(NT):
            i = tt % 2
            if tt + 1 < NT:
                i1 = (tt + 1) % 2
                norm_tile(l, tt + 1, xsrc, xsrc_buf, xt[i1], b_xt[i1], junk, b_junk, st[i1], b_st[i1], gbc, b_gbc, xn[i1], b_xn[i1], part="A")
            sc.newgen(b_hTs[i])
            transposes(i, [b_hTs[i]], lambda half, i=i: hTs[i][:, half * 8:(half + 1) * 8, :])
            sc.op("pool", lambda e, i=i, tt=tt: e.dma_start(out=HTS[:, :, tt * 128:(tt + 1) * 128], in_=hTs[i]),
                  reads=[b_hTs[i]], pwrites=[dbuf["HTS"]], dma=b_hTs[i])
            bk = tt % 2
            for j in range(16):
                sc.op("pe", lambda e, j=j, i=i, bk=bk: e.matmul(banks[bk][0:G4, 0:128], lhsT=wgb[:, j, :], rhs=hTs[i][:, j, :],
                                                              start=(j == 0), stop=(j == 15)),
                      reads=[b_wgb, b_hTs[i]], writes=[bankb[bk]] if j == 0 else (), pwrites=() if j == 0 else [bankb[bk]])
            sc.op("act", lambda e, tt=tt, bk=bk: e.activation(out=GT[:, tt * 128:(tt + 1) * 128], in_=banks[bk][0:G4, 0:128],
                                                            func=AF.Identity, bias=bgt[:, 0:1]),
                  reads=[bankb[bk], b_bgt], pwrites=[b_GT])
            if tt + 1 < NT:
                i1 = (tt + 1) % 2
                norm_tile(l, tt + 1, xsrc, xsrc_buf, xt[i1], b_xt[i1], junk, b_junk, st[i1], b_st[i1], gbc, b_gbc, xn[i1], b_xn[i1], part="B")
        if cfg.stop == "PG0":
            dump("GT", GT, b_GT, [G4, S])
            return
        gi = A.alloc([HPC, S], F32); gf = A.alloc([HPC, S], F32)
        t1 = A.alloc([HPC, S], F32); t2 = A.alloc([HPC, S], F32)
        b_gi, b_gf, b_t1, b_t2 = Buf("gi"), Buf("gf"), Buf("t1"), Buf("t2")
        mpc = A.alloc([HPC, NC], F32); dd = A.alloc([HPC, NC], F32)
        b_mpc, b_dd = Buf("mpc"), Buf("dd")
        pst = banks[0]; b_pst = bankb[0]
        sc.newgen(b_tab)
        for d in range(2):
            rv = (lambda ap: ap) if d == 0 else (lambda ap: ap[:, ::-1])
            sc.op("sp", lambda e, d=d: e.dma_start(out=gi, in_=GT[(2 * d) * HPC:(2 * d + 1) * HPC, :]),
                  reads=[b_GT], writes=[b_gi], dma=b_gi)
            sc.op("sp", lambda e, d=d: e.dma_start(out=gf, in_=GT[(2 * d + 1) * HPC:(2 * d + 2) * HPC, :]),
                  reads=[b_GT], writes=[b_gf], dma=b_gf)
            sc.op("act", lambda e: e.activation(out=t1, in_=gf, func=AF.Exp, scale=-1.0), reads=[b_gf], writes=[b_t1])
            sc.op("act", lambda e: e.activation(out=t2, in_=t1, func=AF.Ln, bias=1.0, scale=1.0), reads=[b_t1], writes=[b_t2])
            sc.op("dve", lambda e, rv=rv: e.tensor_tensor_scan(out=rv(t1), data0=rv(t2), data1=rv(t2), initial=0.0,
                                                             op0=ALU.add, op1=ALU.max), reads=[b_t2], writes=[b_t1])
            sc.op("dve", lambda e: e.tensor_tensor(out=gf, in0=gi, in1=t1, op=ALU.add), reads=[b_gi, b_t1], writes=[b_gf])
            sc.op("dve", lambda e, rv=rv: e.tensor_tensor_scan(out=rv(t2), data0=rv(gf), data1=rv(gf), initial=0.0,
                                                             op0=ALU.max, op1=ALU.max), reads=[b_gf], writes=[b_t2])
            mm3 = t2.rearrange("p (c l) -> p c l", l=L)
            sc.op("dve", lambda e: e.memset(mpc, 0.0), writes=[b_mpc])
            if NC > 1:
                if d == 0:
                    sc.op("dve", lambda e, mm3=mm3: e.tensor_copy(out=mpc[:, 1:NC], in_=mm3[:, 0:NC - 1, L - 1]),
                          reads=[b_t2], pwrites=[b_mpc])
                else:
                    sc.op("dve", lambda e, mm3=mm3: e.tensor_copy(out=mpc[:, 0:NC - 1], in_=mm3[:, 1:NC, 0]),
                          reads=[b_t2], pwrites=[b_mpc])
            sc.op("dve", lambda e: e.memset(dd, 0.0), writes=[b_dd])
            if NC > 1:
                if d == 0:
                    sc.op("dve", lambda e: e.tensor_tensor(out=dd[:, 1:NC], in0=mpc[:, 0:NC - 1], in1=mpc[:, 1:NC], op=ALU.subtract),
                          reads=[b_mpc], pwrites=[b_dd])
                else:
                    sc.op("dve", lambda e: e.tensor_tensor(out=dd[:, 0:NC - 1], in0=mpc[:, 1:NC], in1=mpc[:, 0:NC - 1], op=ALU.subtract),
                          reads=[b_mpc], pwrites=[b_dd])
            sc.op("act", lambda e: e.activation(out=dd, in_=dd, func=AF.Exp), reads=[b_dd], writes=[b_dd])
            mpb = mpc.unsqueeze(2).to_broadcast([HPC, NC, L])
            a3 = gf.rearrange("p (c l) -> p c l", l=L)
            b3 = t1.rearrange("p (c l) -> p c l", l=L)
            sc.op("dve", lambda e, a3=a3, mpb=mpb: e.tensor_tensor(out=a3, in0=a3, in1=mpb, op=ALU.subtract),
                  reads=[b_mpc, b_gf], writes=[b_gf])
            sc.op("dve", lambda e, b3=b3, mpb=mpb: e.tensor_tensor(out=b3, in0=b3, in1=mpb, op=ALU.subtract),
                  reads=[b_mpc, b_t1], writes=[b_t1])
            sc.op("act", lambda e: e.activation(out=gf, in_=gf, func=AF.Exp), reads=[b_gf], writes=[b_gf])
            sc.op("act", lambda e: e.activation(out=t1, in_=t1, func=AF.Exp), reads=[b_t1], writes=[b_t1])
            idh = identf[0:HPC, 0:HPC]
            for t0 in range(0, NT, 64):
                tn = min(64, NT - t0)
                for k in range(tn):
                    tt = t0 + k
                    sc.op("pe", lambda e, tt=tt, k=k: e.transpose(out=pst[:, k * HPC:(k + 1) * HPC],
                                                                  in_=gf[:, tt * 128:(tt + 1) * 128], identity=idh),
                          reads=[b_gf, b_const], writes=[b_pst] if k == 0 else (), pwrites=() if k == 0 else [b_pst])
                sc.op("dve", lambda e, t0=t0, tn=tn, d=d: e.tensor_copy(
                    out=ETK[:, t0:t0 + tn, d * HPC:(d + 1) * HPC],
                    in_=pst[:, 0:tn * HPC].rearrange("p (a b) -> p a b", b=HPC)), reads=[b_pst], pwrites=[b_tab])
            for q, (srcap, b_src) in enumerate([(gf, b_gf), (t1, b_t1)]):
                for c0 in range(0, NC, 64):
                    cn = min(64, NC - c0)
                    for k in range(cn):
                        c = c0 + k
                        sc.op("pe", lambda e, c=c, k=k, srcap=srcap: e.transpose(out=pst[0:L, k * HPC:(k + 1) * HPC],
                                                                                in_=srcap[:, c * L:(c + 1) * L], identity=idh),
                              reads=[b_src, b_const], writes=[b_pst] if k == 0 else (), pwrites=() if k == 0 else [b_pst])
                    col = (2 * q + d) * HPC
                    sc.op("dve", lambda e, c0=c0, cn=cn, col=col: e.tensor_copy(
                        out=ECH[:, c0:c0 + cn, col:col + HPC],
                        in_=pst[0:L, 0:cn * HPC].rearrange("p (a b) -> p a b", b=HPC)), reads=[b_pst], pwrites=[b_tab])
            for h in range(HPC):
                sc.op("pe", lambda e, h=h: e.matmul(pst[:, 0:NC], lhsT=self_[:, h, :], rhs=dd, start=True, stop=True),
                      reads=[b_dd, b_const], writes=[b_pst])
                sc.op("dve", lambda e, h=h, d=d: e.tensor_copy(out=DEC[:, d * HPC + h, :], in_=pst[:, 0:NC]),
                      reads=[b_pst], pwrites=[b_tab])
        sc.op("dve", lambda e: e.tensor_copy(out=ECHB, in_=ECH[:, :, 0:2 * HPC]), reads=[b_tab], pwrites=[b_tab])
        if "ETK1" in cfg.debug:
            dump("GT1", GT, b_GT, [G4, S])
            dump("ETK1", ETK, b_tab, [128, NT, 2 * HPC])
        if cfg.stop == "PG":
            dump("GT", GT, b_GT, [G4, S])
            dump("ETK", ETK, b_tab, [128, NT, 2 * HPC])
            dump("ECH", ECH, b_tab, [L, NC, 4 * HPC])
            dump("DEC", DEC, b_tab, [128, 2 * HPC, NC])
            return
        sc.barrier()

        A.reset()
        gbc_p1 = A.alloc([128, D], F32); b_gbc_p1 = Buf("gbc")
        sc.op("sp", lambda e: e.dma_start(out=gbc_p1, in_=normg[l].partition_broadcast(128)),
              reads=[IN], writes=[b_gbc_p1], dma=b_gbc_p1)
        hT = A.alloc([128, 16, S], BF16)
        b_hT = [Buf("hT%d" % t) for t in range(NT)]
        b_hTld = [Buf("hTld%d" % t) for t in range(NB)]
        wblk = [A.alloc([128, 16, 512], BF16) for _ in range(2)]; b_wblk = [Buf("wblk%d" % i) for i in range(2)]
        NSTG = 4
        stg = [A.alloc([128, 512], BF16) for _ in range(NSTG)]; b_stg = [Buf("stg%d" % i) for i in range(NSTG)]
        stg2 = [A.alloc([128, 512], BF16) for _ in range(2)]; b_stg2 = [Buf("stgb%d" % i) for i in range(2)]
        sgt = [A.alloc([128, 512], F32) for _ in range(2)]; b_sgt = [Buf("sgt%d" % i) for i in range(2)]
        hgp = A.alloc([128, 512], F32); b_hgp = Buf("hgp")
        _xt1 = A.alloc([128, D], F32); _bxt1 = Buf("xt")
        xt_p1 = [_xt1, _xt1]; b_xt_p1 = [_bxt1, _bxt1]
        _xn1 = A.alloc([128, D], BF16); _bxn1 = Buf("xn")
        xn_p1 = [_xn1, _xn1]; b_xn_p1 = [_bxn1, _bxn1]
        junk_p1 = wblk[1][:, 0:4, :].rearrange("p a b -> p (a b)"); b_junk_p1 = b_wblk[1]
        st_p1 = [A.alloc([128, 4], F32) for _ in range(2)]; b_st_p1 = [Buf("st%d" % i) for i in range(2)]
        for i in range(2):
            sc.op("pool", lambda e, i=i: e.memset(st_p1[i][:, 3:4], EPS), writes=[b_st_p1[i]])

        def transposes1(i, tt):
            for half in range(2):
                pb = banks[6 + half][:].bitcast(BF16)
                for j in range(8):
                    jj = half * 8 + j
                    sc.op("pe", lambda e, jj=jj, j=j, pb=pb: e.transpose(out=pb[:, j * 128:(j + 1) * 128],
                                                                        in_=xn_p1[i][:, jj * 128:(jj + 1) * 128],
                                                                        identity=identb),
                          reads=[b_xn_p1[i], b_const], writes=[bankb[6 + half]] if j == 0 else (),
                          pwrites=() if j == 0 else [bankb[6 + half]])
                dst = hT[:, half * 8:(half + 1) * 8, tt * 128:(tt + 1) * 128]
                src = pb.rearrange("p (a b) -> p a b", a=8)
                if half == 0:
                    sc.op("act", lambda e, dst=dst, src=src: e.activation(out=dst, in_=src, func=AF.Copy),
                          reads=[bankb[6 + half]], pwrites=[b_hT[tt]])
                else:
                    sc.op("dve", lambda e, dst=dst, src=src: e.tensor_copy(out=dst, in_=src),
                          reads=[bankb[6 + half]], pwrites=[b_hT[tt]])

        blocks = []
        for i in range(QW // 512):
            blocks.append(("fm", wfm[l][:, i * 512:(i + 1) * 512], ("copy", QT, "QT", i * 512, 1.0)))
        for i in range(QW // 512):
            blocks.append(("fm", wfm[l][:, QW + i * 512:QW + (i + 1) * 512], ("copy", KT, "KT", i * 512, 1.0 / 16.0)))
        for i in range(QW // 512):
            blocks.append(("tm", wtm[l][:, i * 512:(i + 1) * 512], ("copy", KK, "KK", i * 512, 1.0 / 16.0)))
        for i in range(VW // 512):
            blocks.append(("tm", wtm[l][:, QW + i * 512:QW + (i + 1) * 512], ("vscale", None, "VE", i, 1.0)))
        for i in range(VW // 512):
            blocks.append(("tm", wtm[l][:, QW + VW + i * 512:QW + VW + (i + 1) * 512], ("sighg", SO, "SO", i * 512, 1.0)))
        for i in range(VW // 512):
            blocks.append(("tm", wtm[l][:, QW + 2 * VW + i * 512:QW + 2 * VW + (i + 1) * 512], ("silu", SZ, "SZ", i * 512, 1.0)))
        o0 = 2 * QW
        for i in range(CW // 512):
            blocks.append(("fm", wfm[l][:, o0 + i * 512:o0 + (i + 1) * 512], ("copy", XB, "XB", i * 512, 1.0)))
        o0 += CW
        for i in range(CW // 512):
            blocks.append(("fm", wfm[l][:, o0 + i * 512:o0 + (i + 1) * 512], ("silu", ZB, "ZB", i * 512, 1.0)))
        o0 += CW
        for i in range(D // 512):
            blocks.append(("fm", wfm[l][:, o0 + i * 512:o0 + (i + 1) * 512], ("sigmoid", GA, "GA", i * 512, 1.0)))
        o0 += D
        for i in range(D // 512):
            blocks.append(("fm", wfm[l][:, o0 + i * 512:o0 + (i + 1) * 512], ("sigmoid", GB, "GB", i * 512, 1.0)))

        def load_w(bi):
            kind, wsrc, spec = blocks[bi]
            w = bi % 2
            for hh in range(16):
                sc.op("pool", lambda e, w=w, wsrc=wsrc, hh=hh: e.dma_start(
                    out=wblk[w][:, hh, :], in_=wsrc[hh * 128:(hh + 1) * 128, :]),
                    reads=[IN], writes=[b_wblk[w]] if hh == 0 else (), pwrites=() if hh == 0 else [b_wblk[w]],
                    dma=b_wblk[w])

        ucount = [0]

        def evac(spec, ps, b_ps, r0, c0, tm, tt):
            fn, dst, dname, off, scale = spec
            u = ucount[0]; ucount[0] += 1
            if fn == "vscale":
                hd = off
                for d in range(2):
                    s2 = u % 2 if d == 0 else (u + 1) % 2
                    sb = stg2[d]; b_sb = b_stg2[d]
                    col = d * HPC + hd
                    if False:
                        sc.op("act", lambda e, sb=sb, col=col: e.activation(out=sb, in_=ps, func=AF.Identity, scale=ETK[:, tt, col:col + 1]),
                              reads=[b_ps, b_tab], writes=[b_sb])
                    else:
                        sc.op("dve", lambda e, sb=sb, col=col: e.tensor_scalar(out=sb, in0=ps, scalar1=ETK[:, tt, col:col + 1], scalar2=None,
                                                                              op0=ALU.mult), reads=[b_ps, b_tab], writes=[b_sb])
                    sc.op("sp", lambda e, sb=sb, d=d, hd=hd: e.dma_start(out=VE[d, r0:r0 + 128, hd * 512:(hd + 1) * 512], in_=sb),
                          reads=[b_sb], pwrites=[dbuf["VE"]], dma=b_sb)
                return
            k = u % NSTG
            sb = stg[k]; b_sb = b_stg[k]
            if fn == "copy":
                if u % 2 == 0:
                    sc.op("act", lambda e: e.activation(out=sb, in_=ps, func=AF.Identity, scale=scale), reads=[b_ps], writes=[b_sb])
                else:
                    sc.op("dve", lambda e: e.tensor_scalar(out=sb, in0=ps, scalar1=scale, scalar2=None, op0=ALU.mult),
                          reads=[b_ps], writes=[b_sb])
            elif fn == "sigmoid":
                sc.op("act", lambda e: e.activation(out=sb, in_=ps, func=AF.Sigmoid), reads=[b_ps], writes=[b_sb])
            elif fn == "sighg":
                g = sgt[u % 2]; b_g = b_sgt[u % 2]
                sc.op("act", lambda e: e.activation(out=g, in_=ps, func=AF.Sigmoid), reads=[b_ps], writes=[b_g])
                sc.op("dve", lambda e: e.tensor_tensor(out=sb, in0=g, in1=hgp, op=ALU.mult), reads=[b_g, b_hgp], writes=[b_sb])
            elif fn == "silu":
                g = sgt[u % 2]; b_g = b_sgt[u % 2]
                sc.op("act", lambda e: e.activation(out=g, in_=ps, func=AF.Sigmoid), reads=[b_ps], writes=[b_g])
                sc.op("dve", lambda e: e.tensor_tensor(out=sb, in0=ps, in1=g, op=ALU.mult), reads=[b_ps, b_g], writes=[b_sb])
            sc.op("sp", lambda e: e.dma_start(out=dst[r0:r0 + 128, c0:c0 + 512], in_=sb),
                  reads=[b_sb], pwrites=[dbuf[dname]], dma=b_sb)

        pcount = [0]

        def run_block(bi, tts=None):
            kind, wsrc, spec = blocks[bi]
            w = bi % 2
            if spec[0] == "sighg":
                c0_ = spec[3]
                sc.op("sp", lambda e: e.dma_start(out=hgp, in_=headg[l][c0_:c0_ + 512].partition_broadcast(128)),
                      reads=[IN], writes=[b_hgp], dma=b_hgp)
            if kind == "tm":
                for tt in range(NT):
                    bk = pcount[0] % 6; pcount[0] += 1
                    for j in range(16):
                        sc.op("pe", lambda e, j=j, tt=tt, bk=bk: e.matmul(banks[bk][:, :], lhsT=hT[:, j, tt * 128:(tt + 1) * 128],
                                                                        rhs=wblk[w][:, j, :], start=(j == 0), stop=(j == 15)),
                              reads=[b_hT[tt], b_wblk[w]], writes=[bankb[bk]] if j == 0 else (),
                              pwrites=() if j == 0 else [bankb[bk]])
                    off = spec[3]
                    evac(spec, banks[bk][:, :], bankb[bk], tt * 128, off, True, tt)
            else:
                for ct in range(4):
                    for tb in range(NB):
                        bk = pcount[0] % 6; pcount[0] += 1
                        for j in range(16):
                            sc.op("pe", lambda e, j=j, ct=ct, tb=tb, bk=bk: e.matmul(
                                banks[bk][:, :], lhsT=wblk[w][:, j, ct * 128:(ct + 1) * 128],
                                rhs=hT[:, j, tb * 512:(tb + 1) * 512], start=(j == 0), stop=(j == 15)),
                                reads=[b_hT[t] for t in range(tb * 4, tb * 4 + 4)] + [b_wblk[w]],
                                writes=[bankb[bk]] if j == 0 else (), pwrites=() if j == 0 else [bankb[bk]])
                        off = spec[3]
                        evac(spec, banks[bk][:, :], bankb[bk], off + ct * 128, tb * 512, False, None)

        for dn in ["QT", "KT", "KK", "VE", "SO", "SZ", "XB", "ZB", "GA", "GB"]:
            sc.newgen(dbuf[dn])
        load_w(0)
        for tb in range(NB):
            for q in range(4):
                sc.newgen(b_hT[tb * 4 + q])
            bl = [b_hT[tb * 4 + q] for q in range(4)]
            for half in range(2):
                sc.op("sp", lambda e, tb=tb, half=half: e.dma_start(out=hT[:, half * 8:(half + 1) * 8, tb * 512:(tb + 1) * 512],
                                                                  in_=HTS[:, half * 8:(half + 1) * 8, tb * 512:(tb + 1) * 512]),
                      reads=[dbuf["HTS"]], pwrites=bl, dma=b_hTld[tb])
        if cfg.stop == "P1a":
            dump("hT", hT[:, 0, :], b_hT[NT - 1], [128, S], BF16)
            return
        nblk = len(blocks)
        if cfg.stop is not None and cfg.stop.startswith("P1b"):
            nblk = int(cfg.stop[3:])
        for bi in range(nblk):
            if bi + 1 < nblk:
                load_w(bi + 1)
            run_block(bi)
        if cfg.stop is not None:
            return
        if "ETK2" in cfg.debug:
            dump("ETK2", ETK, b_tab, [128, NT, 2 * HPC])
        sc.barrier()
        if cfg.stop == "P1":
            return
        phase2(l)
        sc.barrier()
        if cfg.stop == "P2":
            return
        phase3(l)
        sc.barrier()
        if cfg.stop == "P3":
            return
        phase4a(l)
        sc.barrier()
        if cfg.stop == "P4a":
            return
        phase4b(l, xsrc, xsrc_buf, last)
        sc.barrier()
        return


    def phase2(l):
        A.reset()
        GC = max(1, 256 // L)
        NG = NC // GC
        TG = GC * L
        epsc = A.alloc([L, 1], F32); b_epsc = Buf("epsc")
        sc.op("pool", lambda e: e.memset(epsc, EPS), writes=[b_epsc])

        def mk(shape, dt, name):
            return A.alloc(shape, dt), Buf(name)

        hl_bufs = []
        for hl in range(2):
            Bf = {}
            for s_ in range(2):
                Bf[("qT", s_)] = mk([128, 2, TG], BF16, "qT")
                Bf[("kT", s_)] = mk([128, 2, TG], BF16, "kT")
                Bf[("kk", s_)] = mk([L, GC, 256], BF16, "kk")
                Bf[("ve", s_)] = mk([L, GC, 512], BF16, "ve")
                Bf[("hf", s_)] = mk([L, GC, 512], F32, "hf")
                Bf[("so", s_)] = mk([L, GC, 512], BF16, "so")
                Bf[("sz", s_)] = mk([L, GC, 512], BF16, "sz")
                Bf[("qs", s_)] = mk([128, 2, TG], BF16, "qs")
                Bf[("t3", s_)] = mk([L, 512], F32, "t3")
                Bf[("yT", s_)] = mk([128, 4, TG], BF16, "yT")
                Bf[("swm", s_)] = mk([L, L], BF16, "swm")
                Bf[("hs", s_)] = mk([L, 512], F32, "hs")
                Bf[("ya", s_)] = mk([L, 512], BF16, "ya")
                Bf[("dm", s_)] = mk([L, 2], F32, "dm")
                Bf[("sq", s_)] = mk([L, 4], F32, "sq")
            Bf["U"] = mk([128, 2, 512], F32, "U")
            Bf["Un"] = mk([128, 2], F32, "Un")
            Bf["Cbf"] = mk([128, 2, 512], BF16, "Cbf")
            Bf["nbf"] = mk([128, 2], BF16, "nbf")
            Bf["junk"] = mk([L, 512], BF16, "junk")
            bk = hl * 4
            Bf["ps_s"] = (banks[bk][0:L, 0:L], Buf("ps_s"))
            Bf["ps_den"] = (banks[bk][0:L, L:L + 1], Buf("ps_den"))
            Bf["ps_dn"] = (banks[bk][:, L + 2:L + 4], Buf("ps_dn"))
            Bf["ps_tp"] = (banks[bk][:].bitcast(BF16)[:, 512:512 + 4 * L], Buf("ps_tp"))
            Bf["ps_n"] = (banks[bk + 1][0:L, :], bankb[bk + 1])
            Bf["ps_c0"] = (banks[bk + 2][:, :], bankb[bk + 2])
            Bf["ps_c1"] = (banks[bk + 3][:, :], bankb[bk + 3])
            hl_bufs.append(Bf)

        def load_group(hl, hg, d, g, slot):
            Bf = hl_bufs[hl]
            t0 = g * TG
            qT, b_qT = Bf[("qT", slot)]; kT, b_kT = Bf[("kT", slot)]
            kk, b_kk = Bf[("kk", slot)]; ve, b_ve = Bf[("ve", slot)]
            sc.op("sp", lambda e: e.dma_start(out=qT, in_=QT[hg * 256:(hg + 1) * 256, t0:t0 + TG].rearrange("(k p) t -> p k t", p=128)),
                  reads=[dbuf["QT"]], writes=[b_qT], dma=b_qT)
            sc.op("sp", lambda e: e.dma_start(out=kT, in_=KT[hg * 256:(hg + 1) * 256, t0:t0 + TG].rearrange("(k p) t -> p k t", p=128)),
                  reads=[dbuf["KT"]], writes=[b_kT], dma=b_kT)
            sc.op("sp", lambda e: e.dma_start(out=kk, in_=KK[t0:t0 + TG, hg * 256:(hg + 1) * 256].rearrange("(c p) f -> p c f", p=L)),
                  reads=[dbuf["KK"]], writes=[b_kk], dma=b_kk)
            sc.op("sp", lambda e: e.dma_start(out=ve, in_=VE[d, t0:t0 + TG, hg * 512:(hg + 1) * 512].rearrange("(c p) f -> p c f", p=L)),
                  reads=[dbuf["VE"]], writes=[b_ve], dma=b_ve)
            if d == 1:
                hf, b_hf = Bf[("hf", slot)]; so, b_so = Bf[("so", slot)]; sz, b_sz = Bf[("sz", slot)]
                sc.op("sp", lambda e: e.dma_start(out=hf, in_=HF[t0:t0 + TG, hg * 512:(hg + 1) * 512].rearrange("(c p) f -> p c f", p=L)),
                      reads=[dbuf["HF"]], writes=[b_hf], dma=b_hf)
                sc.op("sp", lambda e: e.dma_start(out=so, in_=SO[t0:t0 + TG, hg * 512:(hg + 1) * 512].rearrange("(c p) f -> p c f", p=L)),
                      reads=[dbuf["SO"]], writes=[b_so], dma=b_so)
                sc.op("sp", lambda e: e.dma_start(out=sz, in_=SZ[t0:t0 + TG, hg * 512:(hg + 1) * 512].rearrange("(c p) f -> p c f", p=L)),
                      reads=[dbuf["SZ"]], writes=[b_sz], dma=b_sz)
            qs, b_qs = Bf[("qs", slot)]
            dinb = DEC[:, d * HPC + hg, g * GC:(g + 1) * GC].unsqueeze(2).to_broadcast([128, GC, L])
            for kt in range(2):
                sc.op("dve", lambda e, kt=kt: e.tensor_tensor(out=qs[:, kt, :].rearrange("p (c t) -> p c t", c=GC),
                                                              in0=qT[:, kt, :].rearrange("p (c t) -> p c t", c=GC), in1=dinb, op=ALU.mult),
                      reads=[b_qT, b_tab], writes=[b_qs] if kt == 0 else (), pwrites=() if kt == 0 else [b_qs])

        def chunk(hl, hg, d, c, slot, ci, stage):
            Bf = hl_bufs[hl]
            qT, b_qT = Bf[("qT", slot)]; kT, b_kT = Bf[("kT", slot)]
            kk, b_kk = Bf[("kk", slot)]; ve, b_ve = Bf[("ve", slot)]
            qs, b_qs = Bf[("qs", slot)]
            U, b_U = Bf["U"]; Un, b_Un = Bf["Un"]; Cbf, b_Cbf = Bf["Cbf"]; nbf, b_nbf = Bf["nbf"]
            ps_s, b_ps_s = Bf["ps_s"]; ps_den, b_ps_den = Bf["ps_den"]; ps_dn, b_ps_dn = Bf["ps_dn"]
            ps_n, b_ps_n = Bf["ps_n"]
            ps_c = [Bf["ps_c0"], Bf["ps_c1"]]
            p2 = c % 2
            swm, b_swm = Bf[("swm", p2)]; dm, b_dm = Bf[("dm", p2)]
            cs = slice(ci * L, (ci + 1) * L)
            din_ = DEC[:, d * HPC + hg, c:c + 1]
            ecol = ECHB[:, c, d * HPC + hg:d * HPC + hg + 1]
            thr = ECH[:, c, (2 + d) * HPC + hg:(2 + d) * HPC + hg + 1]
            if stage == "A":
                for kt in range(2):
                    sc.op("pe", lambda e, kt=kt: e.matmul(ps_s, lhsT=kT[:, kt, cs], rhs=qT[:, kt, cs], start=(kt == 0), stop=(kt == 1)),
                          reads=[b_kT, b_qT], writes=[b_ps_s] if kt == 0 else (), pwrites=() if kt == 0 else [b_ps_s])
                sc.op("dve", lambda e: e.tensor_tensor(out=swm, in0=ps_s, in1=maskb[:, d, :], op=ALU.mult),
                      reads=[b_ps_s, b_const], writes=[b_swm])
                return
            if stage == "B":
                sc.op("act", lambda e: e.activation(out=Cbf, in_=U, func=AF.Copy), reads=[b_U], writes=[b_Cbf])
                sc.op("act", lambda e: e.activation(out=nbf, in_=Un, func=AF.Copy), reads=[b_Un], writes=[b_nbf])
                return
            if stage == "C":
                for kt in range(2):
                    pc, b_pc = ps_c[kt]
                    sc.op("pe", lambda e, kt=kt, pc=pc: e.matmul(pc, lhsT=kk[:, ci, kt * 128:(kt + 1) * 128], rhs=ve[:, ci, :], start=True, stop=True),
                          reads=[b_kk, b_ve], writes=[b_pc])
                for kt in range(2):
                    sc.op("pe", lambda e, kt=kt: e.matmul(ps_dn[:, kt:kt + 1], lhsT=kk[:, ci, kt * 128:(kt + 1) * 128], rhs=ecol, start=True, stop=True),
                          reads=[b_kk, b_tab], writes=[b_ps_dn] if kt == 0 else (), pwrites=() if kt == 0 else [b_ps_dn])
                sc.op("pe", lambda e: e.matmul(ps_n, lhsT=swm, rhs=ve[:, ci, :], start=True, stop=False),
                      reads=[b_swm, b_ve], writes=[b_ps_n])
                for kt in range(2):
                    sc.op("pe", lambda e, kt=kt: e.matmul(ps_n, lhsT=qs[:, kt, cs], rhs=Cbf[:, kt, :], start=False, stop=(kt == 1)),
                          reads=[b_qs, b_Cbf], pwrites=[b_ps_n])
                sc.op("pe", lambda e: e.matmul(ps_den, lhsT=swm, rhs=ecol, start=True, stop=False),
                      reads=[b_swm, b_tab], writes=[b_ps_den])
                for kt in range(2):
                    sc.op("pe", lambda e, kt=kt: e.matmul(ps_den, lhsT=qs[:, kt, cs], rhs=nbf[:, kt:kt + 1], start=False, stop=(kt == 1)),
                          reads=[b_qs, b_nbf], pwrites=[b_ps_den])
                return
            for kt in range(2):
                pc, b_pc = ps_c[kt]
                sc.op("dve", lambda e, kt=kt, pc=pc: e.scalar_tensor_tensor(out=U[:, kt, :], in0=U[:, kt, :], scalar=din_, in1=pc,
                                                                          op0=ALU.mult, op1=ALU.add),
                      reads=[b_U, b_pc, b_tab], pwrites=[b_U])
            sc.op("dve", lambda e: e.scalar_tensor_tensor(out=Un, in0=Un, scalar=din_, in1=ps_dn, op0=ALU.mult, op1=ALU.add),
                  reads=[b_Un, b_ps_dn, b_tab], pwrites=[b_Un])
            sc.op("dve", lambda e: e.tensor_tensor(out=dm[:, 0:1], in0=ps_den, in1=thr, op=ALU.max),
                  reads=[b_ps_den, b_tab], writes=[b_dm])
            sc.op("dve", lambda e: e.scalar_tensor_tensor(out=dm[:, 0:1], in0=ps_den, scalar=-1.0, in1=dm[:, 0:1], op0=ALU.mult, op1=ALU.max),
                  reads=[b_ps_den, b_dm], pwrites=[b_dm])
            sc.op("dve", lambda e: e.reciprocal(out=dm[:, 1:2], in_=dm[:, 0:1]), reads=[b_dm], pwrites=[b_dm])
            if d == 0:
                hf, b_hf = Bf[("hf", slot)]
                sc.op("dve", lambda e: e.tensor_scalar(out=hf[:, ci, :], in0=ps_n, scalar1=dm[:, 1:2], scalar2=None, op0=ALU.mult),
                      reads=[b_ps_n, b_dm], pwrites=[b_hf])
            else:
                hf, b_hf = Bf[("hf", slot)]; so, b_so = Bf[("so", slot)]; sz, b_sz = Bf[("sz", slot)]
                t3, b_t3 = Bf[("t3", p2)]
                hs, b_hs = Bf[("hs", p2)]; ya, b_ya = Bf[("ya", p2)]; sq, b_sq = Bf[("sq", p2)]
                junk, b_junk = Bf["junk"]
                yT, b_yT = Bf[("yT", slot)]
                ps_tp, b_ps_tp = Bf["ps_tp"]
                sc.op("dve", lambda e: e.scalar_tensor_tensor(out=hs, in0=ps_n, scalar=dm[:, 1:2], in1=hf[:, ci, :], op0=ALU.mult, op1=ALU.add),
                      reads=[b_ps_n, b_dm, b_hf], writes=[b_hs])
                sc.op("act", lambda e: e.activation(out=junk, in_=hs, func=AF.Square), reads=[b_hs], writes=[b_junk])
                sc.op("dve", lambda e: e.reduce_sum(out=sq[:, 0:1], in_=junk, axis=AX.X), reads=[b_junk], writes=[b_sq])
                sc.op("act", lambda e: e.activation(out=sq[:, 1:2], in_=sq[:, 0:1], func=AF.Sqrt, bias=epsc[:, 0:1], scale=1.0 / 512.0),
                      reads=[b_sq, b_epsc], pwrites=[b_sq])
                sc.op("dve", lambda e: e.reciprocal(out=sq[:, 2:3], in_=sq[:, 1:2]), reads=[b_sq], pwrites=[b_sq])
                sc.op("dve", lambda e: e.scalar_tensor_tensor(out=t3, in0=hs, scalar=sq[:, 2:3], in1=so[:, ci, :], op0=ALU.mult, op1=ALU.mult),
                      reads=[b_hs, b_sq, b_so], writes=[b_t3])
                sc.op("dve", lambda e: e.tensor_tensor(out=ya, in0=t3, in1=sz[:, ci, :], op=ALU.mult),
                      reads=[b_t3, b_sz], writes=[b_ya])
                for f in range(4):
                    sc.op("pe", lambda e, f=f: e.transpose(out=ps_tp[:, f * L:(f + 1) * L], in_=ya[:, f * 128:(f + 1) * 128],
                                                           identity=identb[0:L, 0:L]),
                          reads=[b_ya, b_const], writes=[b_ps_tp] if f == 0 else (), pwrites=() if f == 0 else [b_ps_tp])
                sc.op("act", lambda e: e.activation(out=yT[:, :, cs], in_=ps_tp.rearrange("p (f t) -> p f t", f=4), func=AF.Copy),
                      reads=[b_ps_tp], pwrites=[b_yT])

        def store_group(hl, hg, d, g, slot):
            Bf = hl_bufs[hl]
            t0 = g * TG
            if d == 0:
                hf, b_hf = Bf[("hf", slot)]
                sc.op("sp", lambda e: e.dma_start(out=HF[t0:t0 + TG, hg * 512:(hg + 1) * 512].rearrange("(c p) f -> p c f", p=L), in_=hf),
                      reads=[b_hf], pwrites=[dbuf["HF"]], dma=b_hf)
            else:
                yT, b_yT = Bf[("yT", slot)]
                sc.op("sp", lambda e: e.dma_start(out=YAT[hg * 512:(hg + 1) * 512, t0:t0 + TG].rearrange("(f p) t -> p f t", p=128), in_=yT),
                      reads=[b_yT], pwrites=[dbuf["YAT"]], dma=b_yT)

        sc.newgen(dbuf["YAT"])
        for hp in range(HPC // 2):
            for d in range(2):
                if d == 0 and hp == 0:
                    sc.newgen(dbuf["HF"])
                order = list(range(NG)) if d == 0 else list(range(NG - 1, -1, -1))
                for hl in range(2):
                    U, b_U = hl_bufs[hl]["U"]; Un, b_Un = hl_bufs[hl]["Un"]
                    sc.op("dve", lambda e, U=U: e.memset(U, 0.0), writes=[b_U])
                    sc.op("dve", lambda e, Un=Un: e.memset(Un, 0.0), writes=[b_Un])
                    load_group(hl, 2 * hp + hl, d, order[0], 0)
                steps = []
                for si, g in enumerate(order):
                    cis = list(range(GC)) if d == 0 else list(range(GC - 1, -1, -1))
                    for k, ci in enumerate(cis):
                        steps.append((si, g, si % 2, ci, k == 0, k == GC - 1))

                def emit(st, stage):
                    si_, g_, slot_, ci_, _, _ = st
                    for hl in range(2):
                        chunk(hl, 2 * hp + hl, d, g_ * GC + ci_, slot_, ci_, stage)

                emit(steps[0], "A")
                for idx, st in enumerate(steps):
                    si, g, slot, ci, first, lastc = st
                    if first:
                        if si + 1 < NG:
                            for hl in range(2):
                                load_group(hl, 2 * hp + hl, d, order[si + 1], (si + 1) % 2)
                        for hl in range(2):
                            sc.newgen(hl_bufs[hl][("yT" if d == 1 else "hf", slot)][1])
                    emit(st, "B")
                    emit(st, "C")
                    if idx + 1 < len(steps):
                        emit(steps[idx + 1], "A")
                    emit(st, "D")
                    if lastc:
                        for hl in range(2):
                            store_group(hl, 2 * hp + hl, d, g, slot)

    def phase3(l):
        A.reset()
        CV = A.alloc([128, NCT, 12], F32); b_CV = Buf("CV")
        sc.op("sp", lambda e: e.dma_start(out=CV, in_=cvec[l].rearrange("(c p) k -> p c k", p=128)),
              reads=[IN], writes=[b_CV], dma=b_CV)
        C1 = A.alloc([128, NCT, 2], F32); b_C1 = Buf("C1")
        sc.op("act", lambda e: e.activation(out=C1, in_=CV[:, :, 9:11], func=AF.Exp, scale=-1.0), reads=[b_CV], writes=[b_C1])
        sc.op("act", lambda e: e.activation(out=C1, in_=C1, func=AF.Ln, bias=1.0, scale=1.0), reads=[b_C1], writes=[b_C1])
        sc.op("dve", lambda e: e.tensor_scalar(out=C1, in0=C1, scalar1=-8.0, scalar2=None, op0=ALU.mult), reads=[b_C1], writes=[b_C1])
        xb = [A.alloc([128, S], BF16) for _ in range(2)]; b_xb = [Buf("xb%d" % i) for i in range(2)]
        szb = [A.alloc([128, S], BF16) for _ in range(2)]; b_szb = [Buf("szb%d" % i) for i in range(2)]
        wr = [A.alloc([128, 4, 128], BF16) for _ in range(2)]; b_wr = [Buf("wr%d" % i) for i in range(2)]
        xc = A.alloc([128, S], F32); b_xc = Buf("xc")
        xcb = A.alloc([128, S], BF16); b_xcb = Buf("xcb")
        rr = A.alloc([128, S], F32); b_rr = Buf("rr")
        ig = A.alloc([128, S], F32); b_ig = Buf("ig")
        a2 = A.alloc([128, S], F32); b_a2 = Buf("a2")
        hh = [A.alloc([128, S], F32) for _ in range(2)]; b_hh = [Buf("hh%d" % i) for i in range(2)]
        ybt = [A.alloc([128, S], BF16) for _ in range(2)]; b_ybt = [Buf("ybt%d" % i) for i in range(2)]
        pc = [0]
        sc.newgen(dbuf["YBT"])

        def load_ct(ct):
            sl = ct % 2
            sc.op("sp", lambda e: e.dma_start(out=xb[sl], in_=XB[ct * 128:(ct + 1) * 128, :]), reads=[dbuf["XB"]], writes=[b_xb[sl]], dma=b_xb[sl])
            sc.op("sp", lambda e: e.dma_start(out=szb[sl], in_=ZB[ct * 128:(ct + 1) * 128, :]), reads=[dbuf["ZB"]], writes=[b_szb[sl]], dma=b_szb[sl])
            sc.op("pool", lambda e: e.dma_start(out=wr[sl], in_=wrg[l][:, ct].rearrange("g c d -> c g d")),
                  reads=[IN], writes=[b_wr[sl]], dma=b_wr[sl])

        load_ct(0)
        for ct in range(NCT):
            sl = ct % 2
            if ct + 1 < NCT:
                load_ct(ct + 1)
            x_ = xb[sl]; bx = b_xb[sl]
            cw = lambda k, ct=ct: CV[:, ct, k:k + 1]
            sc.op("dve", lambda e, x_=x_, cw=cw: e.tensor_scalar(out=xc, in0=x_, scalar1=cw(2), scalar2=cw(4), op0=ALU.mult, op1=ALU.add),
                  reads=[bx, b_CV], writes=[b_xc])
            sc.op("dve", lambda e, x_=x_, cw=cw: e.scalar_tensor_tensor(out=xc[:, 1:S], in0=x_[:, 0:S - 1], scalar=cw(1), in1=xc[:, 1:S],
                                                                        op0=ALU.mult, op1=ALU.add), reads=[bx, b_CV, b_xc], writes=[b_xc])
            sc.op("dve", lambda e, x_=x_, cw=cw: e.scalar_tensor_tensor(out=xc[:, 2:S], in0=x_[:, 0:S - 2], scalar=cw(0), in1=xc[:, 2:S],
                                                                        op0=ALU.mult, op1=ALU.add), reads=[bx, b_CV, b_xc], writes=[b_xc])
            sc.op("dve", lambda e, x_=x_, cw=cw: e.scalar_tensor_tensor(out=xc[:, 0:S - 1], in0=x_[:, 1:S], scalar=cw(3), in1=xc[:, 0:S - 1],
                                                                        op0=ALU.mult, op1=ALU.add), reads=[bx, b_CV, b_xc], writes=[b_xc])
            sc.op("act", lambda e: e.activation(out=xcb, in_=xc, func=AF.Copy), reads=[b_xc], writes=[b_xcb])
            for d in range(2):
                sc.newgen(b_rr); sc.newgen(b_ig)
                for tb in range(NB):
                    for gate in range(2):
                        bk = pc[0] % 8; pc[0] += 1
                        dst, b_dst = (rr, b_rr) if gate == 0 else (ig, b_ig)
                        sc.op("pe", lambda e, bk=bk, d=d, gate=gate, tb=tb, sl=sl: e.matmul(
                            banks[bk][:, :], lhsT=wr[sl][:, 2 * d + gate, :], rhs=xcb[:, tb * 512:(tb + 1) * 512], start=True, stop=True),
                            reads=[b_wr[sl], b_xcb], writes=[bankb[bk]])
                        sc.op("act", lambda e, bk=bk, d=d, gate=gate, tb=tb, dst=dst, cw=cw: e.activation(
                            out=dst[:, tb * 512:(tb + 1) * 512], in_=banks[bk][:, :], func=AF.Sigmoid, bias=cw(5 + 2 * d + gate)),
                            reads=[bankb[bk], b_CV], pwrites=[b_dst])
                c1 = C1[:, ct, d:d + 1]
                sc.op("dve", lambda e, c1=c1: e.tensor_scalar(out=rr, in0=rr, scalar1=c1, scalar2=None, op0=ALU.mult),
                      reads=[b_rr, b_C1], writes=[b_rr])
                sc.op("act", lambda e: e.activation(out=rr, in_=rr, func=AF.Exp), reads=[b_rr], writes=[b_rr])
                sc.op("act", lambda e: e.activation(out=a2, in_=rr, func=AF.Square), reads=[b_rr], writes=[b_a2])
                sc.op("act", lambda e, cw=cw: e.activation(out=a2, in_=a2, func=AF.Sqrt, bias=cw(11), scale=-1.0),
                      reads=[b_a2, b_CV], writes=[b_a2])
                sc.op("dve", lambda e: e.tensor_tensor(out=ig, in0=ig, in1=xc, op=ALU.mult), reads=[b_ig, b_xc], writes=[b_ig])
                sc.op("dve", lambda e: e.tensor_tensor(out=ig, in0=ig, in1=a2, op=ALU.mult), reads=[b_ig, b_a2], writes=[b_ig])
                rv = (lambda ap: ap) if d == 0 else (lambda ap: ap[:, ::-1])
                sc.op("dve", lambda e, d=d, rv=rv: e.tensor_tensor_scan(out=rv(hh[d]), data0=rv(rr), data1=rv(ig), initial=0.0,
                                                                      op0=ALU.mult, op1=ALU.add),
                      reads=[b_rr, b_ig], writes=[b_hh[d]])
            sc.op("dve", lambda e: e.tensor_tensor(out=hh[0], in0=hh[0], in1=hh[1], op=ALU.add), reads=[b_hh[0], b_hh[1]], writes=[b_hh[0]])
            sc.op("dve", lambda e, sl=sl: e.tensor_tensor(out=ybt[sl], in0=hh[0], in1=szb[sl], op=ALU.mult),
                  reads=[b_hh[0], b_szb[sl]], writes=[b_ybt[sl]])
            sc.op("sp", lambda e, sl=sl, ct=ct: e.dma_start(out=YBT[ct * 128:(ct + 1) * 128, :], in_=ybt[sl]),
                  reads=[b_ybt[sl]], pwrites=[dbuf["YBT"]], dma=b_ybt[sl])

    def phase4a(l):
        A.reset()
        NKA = VW // 128; NKB = CW // 128
        Wa = A.alloc([128, NKA, D], BF16); b_Wa = Buf("Wa")
        Wb = A.alloc([128, NKB, D], BF16); b_Wb = Buf("Wb")
        sc.newgen(b_Wa); sc.newgen(b_Wb)
        for k in range(NKA):
            sc.op("pool", lambda e, k=k: e.dma_start(out=Wa[:, k, :], in_=wa[l][k * 128:(k + 1) * 128, :]), reads=[IN], pwrites=[b_Wa], dma=b_Wa)
        for k in range(NKB):
            sc.op("pool", lambda e, k=k: e.dma_start(out=Wb[:, k, :], in_=wb[l][k * 128:(k + 1) * 128, :]), reads=[IN], pwrites=[b_Wb], dma=b_Wb)
        ya_b = [A.alloc([128, NKA, 512], BF16) for _ in range(2)]; b_ya_b = [Buf("ya_b%d" % i) for i in range(2)]
        yb_b = [A.alloc([128, NKB, 512], BF16) for _ in range(2)]; b_yb_b = [Buf("yb_b%d" % i) for i in range(2)]
        ga_b = [A.alloc([128, 16, 512], BF16) for _ in range(2)]; b_ga_b = [Buf("ga_b%d" % i) for i in range(2)]
        gb_b = [A.alloc([128, 16, 512], BF16) for _ in range(2)]; b_gb_b = [Buf("gb_b%d" % i) for i in range(2)]
        m1 = [A.alloc([128, 512], F32) for _ in range(2)]; b_m1 = [Buf("m1%d" % i) for i in range(2)]
        m2 = [A.alloc([128, 512], F32) for _ in range(2)]; b_m2 = [Buf("m2%d" % i) for i in range(2)]
        mt = [A.alloc([128, 512], BF16) for _ in range(2)]; b_mt = [Buf("mt%d" % i) for i in range(2)]
        sc.newgen(dbuf["MT"])
        u = [0]

        def load_tb(tb):
            sl = tb % 2
            ts = slice(tb * 512, (tb + 1) * 512)
            sc.op("sp", lambda e: e.dma_start(out=ya_b[sl], in_=YAT[:, ts].rearrange("(k p) t -> p k t", p=128)),
                  reads=[dbuf["YAT"]], writes=[b_ya_b[sl]], dma=b_ya_b[sl])
            sc.op("sp", lambda e: e.dma_start(out=yb_b[sl], in_=YBT[:, ts].rearrange("(k p) t -> p k t", p=128)),
                  reads=[dbuf["YBT"]], writes=[b_yb_b[sl]], dma=b_yb_b[sl])
            sc.op("sp", lambda e: e.dma_start(out=ga_b[sl], in_=GA[:, ts].rearrange("(k p) t -> p k t", p=128)),
                  reads=[dbuf["GA"]], writes=[b_ga_b[sl]], dma=b_ga_b[sl])
            sc.op("sp", lambda e: e.dma_start(out=gb_b[sl], in_=GB[:, ts].rearrange("(k p) t -> p k t", p=128)),
                  reads=[dbuf["GB"]], writes=[b_gb_b[sl]], dma=b_gb_b[sl])

        load_tb(0)
        for tb in range(NB):
            sl = tb % 2
            ts = slice(tb * 512, (tb + 1) * 512)
            if tb + 1 < NB:
                load_tb(tb + 1)
            for dt in range(16):
                i = u[0] % 2; u[0] += 1
                bka = (2 * u[0]) % 8; bkb = (2 * u[0] + 1) % 8
                for k in range(NKA):
                    sc.op("pe", lambda e, k=k, dt=dt, bka=bka, sl=sl: e.matmul(banks[bka][:, :], lhsT=Wa[:, k, dt * 128:(dt + 1) * 128], rhs=ya_b[sl][:, k, :],
                                                                             start=(k == 0), stop=(k == NKA - 1)),
                          reads=[b_Wa, b_ya_b[sl]], writes=[bankb[bka]] if k == 0 else (), pwrites=() if k == 0 else [bankb[bka]])
                for k in range(NKB):
                    sc.op("pe", lambda e, k=k, dt=dt, bkb=bkb, sl=sl: e.matmul(banks[bkb][:, :], lhsT=Wb[:, k, dt * 128:(dt + 1) * 128], rhs=yb_b[sl][:, k, :],
                                                                             start=(k == 0), stop=(k == NKB - 1)),
                          reads=[b_Wb, b_yb_b[sl]], writes=[bankb[bkb]] if k == 0 else (), pwrites=() if k == 0 else [bankb[bkb]])
                sc.op("dve", lambda e, i=i, dt=dt, bka=bka, sl=sl: e.tensor_tensor(out=m1[i], in0=banks[bka][:, :], in1=ga_b[sl][:, dt, :], op=ALU.mult),
                      reads=[bankb[bka], b_ga_b[sl]], writes=[b_m1[i]])
                sc.op("dve", lambda e, i=i, dt=dt, bkb=bkb, sl=sl: e.tensor_tensor(out=m2[i], in0=banks[bkb][:, :], in1=gb_b[sl][:, dt, :], op=ALU.mult),
                      reads=[bankb[bkb], b_gb_b[sl]], writes=[b_m2[i]])
                sc.op("dve", lambda e, i=i: e.tensor_tensor(out=mt[i], in0=m1[i], in1=m2[i], op=ALU.add),
                      reads=[b_m1[i], b_m2[i]], writes=[b_mt[i]])
                sc.op("sp", lambda e, i=i, dt=dt, ts=ts: e.dma_start(out=MT[dt * 128:(dt + 1) * 128, ts], in_=mt[i]),
                      reads=[b_mt[i]], pwrites=[dbuf["MT"]], dma=b_mt[i])

    def phase4b(l, xsrc, xsrc_buf, last):
        A.reset()
        Wo = A.alloc([128, 16, D], BF16); b_Wo = Buf("Wo")
        sc.newgen(b_Wo)
        for k in range(16):
            sc.op("pool", lambda e, k=k: e.dma_start(out=Wo[:, k, :], in_=wout[l][k * 128:(k + 1) * 128, :]), reads=[IN], pwrites=[b_Wo], dma=b_Wo)
        fgb = A.alloc([128, D], F32); b_fgb = Buf("fgb")
        if last:
            sc.op("sp", lambda e: e.dma_start(out=fgb, in_=finalg.partition_broadcast(128)), reads=[IN], writes=[b_fgb], dma=b_fgb)
        mtb = [A.alloc([128, 16, 512], BF16) for _ in range(2)]; b_mtb = [Buf("mtb%d" % i) for i in range(2)]
        xt = [A.alloc([128, D], F32) for _ in range(2)]; b_xt = [Buf("xt4%d" % i) for i in range(2)]
        xo = [A.alloc([128, D], F32) for _ in range(2)]; b_xo = [Buf("xo%d" % i) for i in range(2)]
        junk = A.alloc([128, D], BF16); b_junk = Buf("junk4")
        st = [A.alloc([128, 4], F32) for _ in range(2)]; b_st = [Buf("st4%d" % i) for i in range(2)]
        for i in range(2):
            sc.op("pool", lambda e, i=i: e.memset(st[i][:, 3:4], EPS), writes=[b_st[i]])
        SPLIT = cfg.split == 2
        b_cc = Buf("cc")
        fuse5 = SPLIT and last
        if SPLIT:
            dst, dname = PP, "PP"
            last = False
        else:
            dst, dname = (out, "OUT") if last else (X1, "X1")
        sc.newgen(dbuf[dname])
        if SPLIT:
            sc.newgen(dbuf["PS%d" % l])
        pc = [0]
        if fuse5:
            xt5 = [A.alloc([128, D], F32) for _ in range(2)]; b_xt5 = [Buf("xt5%d" % i) for i in range(2)]
            xo5 = [A.alloc([128, D], F32) for _ in range(2)]; b_xo5 = [Buf("xo5%d" % i) for i in range(2)]
            PS5 = PSL[l]
            b_PSblk = [Buf("PSblk%d" % i) for i in range(NB)]
            n_st1 = [0]
            sc.newgen(dbuf["OUT"])

            def f_st1(tt):
                i = tt % 2
                sc.op("pool", lambda e: e.dma_start(out=xt5[i], in_=PS5[tt * 128:(tt + 1) * 128, :]),
                      reads=[b_PSblk[tt // 4]], writes=[b_xt5[i]], dma=b_xt5[i])
                sc.op("act", lambda e: e.activation(out=junk, in_=xt5[i], func=AF.Square), reads=[b_xt5[i]], writes=[b_junk])
                sc.op("dve", lambda e: e.reduce_sum(out=st[i][:, 0:1], in_=junk, axis=AX.X), reads=[b_junk], writes=[b_st[i]])
                sc.op("act", lambda e: e.activation(out=st[i][:, 1:2], in_=st[i][:, 0:1], func=AF.Sqrt, bias=st[i][:, 3:4], scale=1.0 / D),
                      reads=[b_st[i]], pwrites=[b_st[i]])
                sc.op("dve", lambda e: e.reciprocal(out=st[i][:, 2:3], in_=st[i][:, 1:2]), reads=[b_st[i]], pwrites=[b_st[i]])

            def f_st2(tt):
                i = tt % 2
                sc.op("dve", lambda e: e.scalar_tensor_tensor(out=xo5[i], in0=xt5[i], scalar=st[i][:, 2:3], in1=fgb, op0=ALU.mult, op1=ALU.mult),
                      reads=[b_xt5[i], b_st[i], b_fgb], writes=[b_xo5[i]])
                sc.op("act", lambda e: e.dma_start(out=out[tt * 128:(tt + 1) * 128, :], in_=xo5[i]),
                      reads=[b_xo5[i]], pwrites=[dbuf["OUT"]], dma=b_xo5[i])

            def f_tile(p):
                while n_st1[0] <= min(p + 1, NT - 1):
                    f_st1(n_st1[0]); n_st1[0] += 1
                f_st2(p)

        def load_mt(tb):
            sl = tb % 2
            sc.op("act", lambda e: e.dma_start(out=mtb[sl], in_=MT[:, tb * 512:(tb + 1) * 512].rearrange("(k p) t -> p k t", p=128)),
                  reads=[dbuf["MT"]], writes=[b_mtb[sl]], dma=b_mtb[sl])

        load_mt(0)
        for tb in range(NB):
            sl = tb % 2
            if tb + 1 < NB:
                load_mt(tb + 1)
            for q in range(4):
                tt = tb * 4 + q
                i = tt % 2
                if tt == 0:
                    sc.op("sp", lambda e: e.dma_start(out=xt[0], in_=xsrc[0:128, :]),
                          reads=[xsrc_buf], writes=[b_xt[0]], dma=b_xt[0])
                if tt + 1 < NT:
                    sc.op("sp", lambda e, tt=tt: e.dma_start(out=xt[(tt + 1) % 2], in_=xsrc[(tt + 1) * 128:(tt + 2) * 128, :]),
                          reads=[xsrc_buf], writes=[b_xt[(tt + 1) % 2]], dma=b_xt[(tt + 1) % 2])
                sc.newgen(b_xo[i])
                for eb in range(4):
                    bk = pc[0] % 8; pc[0] += 1
                    for k in range(16):
                        sc.op("pe", lambda e, k=k, q=q, eb=eb, bk=bk, sl=sl: e.matmul(
                            banks[bk][:, :], lhsT=mtb[sl][:, k, q * 128:(q + 1) * 128], rhs=Wo[:, k, eb * 512:(eb + 1) * 512],
                            start=(k == 0), stop=(k == 15)),
                            reads=[b_mtb[sl], b_Wo], writes=[bankb[bk]] if k == 0 else (), pwrites=() if k == 0 else [bankb[bk]])
                    if SPLIT:
                        sc.op("dve", lambda e, eb=eb, bk=bk, i=i: e.scalar_tensor_tensor(
                            out=xo[i][:, eb * 512:(eb + 1) * 512], in0=xt[i][:, eb * 512:(eb + 1) * 512], scalar=flagt[:, 0:1],
                            in1=banks[bk][:, :], op0=ALU.mult, op1=ALU.add),
                            reads=[bankb[bk], b_xt[i], b_const], pwrites=[b_xo[i]])
                    else:
                        sc.op("dve", lambda e, eb=eb, bk=bk, i=i: e.tensor_tensor(out=xo[i][:, eb * 512:(eb + 1) * 512], in0=banks[bk][:, :],
                                                                                 in1=xt[i][:, eb * 512:(eb + 1) * 512], op=ALU.add),
                              reads=[bankb[bk], b_xt[i]], pwrites=[b_xo[i]])
                if last:
                    sc.op("act", lambda e, i=i: e.activation(out=junk, in_=xo[i], func=AF.Square), reads=[b_xo[i]], writes=[b_junk])
                    sc.op("dve", lambda e, i=i: e.reduce_sum(out=st[i][:, 0:1], in_=junk, axis=AX.X), reads=[b_junk], writes=[b_st[i]])
                    sc.op("act", lambda e, i=i: e.activation(out=st[i][:, 1:2], in_=st[i][:, 0:1], func=AF.Sqrt, bias=st[i][:, 3:4], scale=1.0 / D),
                          reads=[b_st[i]], pwrites=[b_st[i]])
                    sc.op("dve", lambda e, i=i: e.reciprocal(out=st[i][:, 2:3], in_=st[i][:, 1:2]), reads=[b_st[i]], pwrites=[b_st[i]])
                    sc.op("dve", lambda e, i=i: e.scalar_tensor_tensor(out=xo[i], in0=xo[i], scalar=st[i][:, 2:3], in1=fgb, op0=ALU.mult, op1=ALU.mult),
                          reads=[b_xo[i], b_st[i], b_fgb], writes=[b_xo[i]])
                sc.op("sp", lambda e, tt=tt, i=i: e.dma_start(out=dst[tt * 128:(tt + 1) * 128, :], in_=xo[i]),
                      reads=[b_xo[i]], pwrites=[dbuf[dname]], dma=b_xo[i])
                if fuse5 and tt >= 8:
                    f_tile(tt - 8)
            if SPLIT:
                PSl = PSL[l]
                sc.op("pool", lambda e, tb=tb, PSl=PSl: e.collective_compute(
                    "AllReduce", ALU.add, replica_groups=cfg.groups,
                    ins=[PP[tb * 512:(tb + 1) * 512, :].opt()], outs=[PSl[tb * 512:(tb + 1) * 512, :].opt()]),
                    reads=[dbuf["PP"]], pwrites=[dbuf["PS%d" % l]] + ([b_PSblk[tb]] if fuse5 else []), dma=b_cc, dinc=1)
        if fuse5:
            for p in range(max(0, NT - 8), NT):
                f_tile(p)

    def phase5(src, src_buf):
        A.reset()
        fgb = A.alloc([128, D], F32); b_fgb = Buf("fgb5")
        sc.op("sp", lambda e: e.dma_start(out=fgb, in_=finalg.partition_broadcast(128)), reads=[IN], writes=[b_fgb], dma=b_fgb)
        xt = [A.alloc([128, D], F32) for _ in range(2)]; b_xt = [Buf("xt5%d" % i) for i in range(2)]
        xo = [A.alloc([128, D], F32) for _ in range(2)]; b_xo = [Buf("xo5%d" % i) for i in range(2)]
        junk = A.alloc([128, D], BF16); b_junk = Buf("junk5")
        st = [A.alloc([128, 4], F32) for _ in range(2)]; b_st = [Buf("st5%d" % i) for i in range(2)]
        for i in range(2):
            sc.op("pool", lambda e, i=i: e.memset(st[i][:, 3:4], EPS), writes=[b_st[i]])
        sc.newgen(dbuf["OUT"])

        def st1(tt):
            i = tt % 2
            sc.op("sp", lambda e: e.dma_start(out=xt[i], in_=src[tt * 128:(tt + 1) * 128, :]),
                  reads=[src_buf], writes=[b_xt[i]], dma=b_xt[i])
            sc.op("act", lambda e: e.activation(out=junk, in_=xt[i], func=AF.Square), reads=[b_xt[i]], writes=[b_junk])
            sc.op("dve", lambda e: e.reduce_sum(out=st[i][:, 0:1], in_=junk, axis=AX.X), reads=[b_junk], writes=[b_st[i]])
            sc.op("act", lambda e: e.activation(out=st[i][:, 1:2], in_=st[i][:, 0:1], func=AF.Sqrt, bias=st[i][:, 3:4], scale=1.0 / D),
                  reads=[b_st[i]], pwrites=[b_st[i]])
            sc.op("dve", lambda e: e.reciprocal(out=st[i][:, 2:3], in_=st[i][:, 1:2]), reads=[b_st[i]], pwrites=[b_st[i]])

        def st2(tt):
            i = tt % 2
            sc.op("dve", lambda e: e.scalar_tensor_tensor(out=xo[i], in0=xt[i], scalar=st[i][:, 2:3], in1=fgb, op0=ALU.mult, op1=ALU.mult),
                  reads=[b_xt[i], b_st[i], b_fgb], writes=[b_xo[i]])
            sc.op("sp", lambda e: e.dma_start(out=out[tt * 128:(tt + 1) * 128, :], in_=xo[i]),
                  reads=[b_xo[i]], pwrites=[dbuf["OUT"]], dma=b_xo[i])

        st1(0)
        for tt in range(NT):
            if tt + 1 < NT:
                st1(tt + 1)
            st2(tt)

    xsrc, xsrc_buf = x_in, IN
    for l in range(DEPTH):
        layer(l, xsrc, xsrc_buf, l == DEPTH - 1)
        if cfg.stop is not None:
            break
        if cfg.split == 2:
            xsrc, xsrc_buf = PSL[l], dbuf["PS%d" % l]
        else:
            xsrc, xsrc_buf = X1, dbuf["X1"]
    if cfg.split == 2 and cfg.stop is None:
        if "PPd" in cfg.debug:
            dump("PP", PP, dbuf["PP"], [S, D])
            dump("PS", xsrc, xsrc_buf, [S, D])
        pass
    sc.barrier()

    with nc.Block() as block:
        @block.tensor
        def _(e):
            sc.replay(e, "pe")

        @block.scalar
        def _(e):
            sc.replay(e, "act")

        @block.vector
        def _(e):
            sc.replay(e, "dve")

        @block.gpsimd
        def _(e):
            sc.replay(e, "pool")

        @block.sync
        def _(e):
            sc.replay(e, "sp")
    return nc


def make_consts(HPC):
    ident = np.eye(128, dtype=np.float32)
    mask = np.zeros((L, 2, L), np.float32)
    s_idx = np.arange(L)[:, None]
    j_idx = np.arange(L)[None, :]
    mask[:, 0, :] = (s_idx <= j_idx)
    mask[:, 1, :] = (s_idx >= j_idx)
    sel = np.zeros((HPC, HPC, 128), np.float32)
    for h in range(HPC):
        sel[h, h, :] = 1.0
    return ident, mask, sel


def prep_core(inp, b, heads, cts, S, depth):
    HPC = len(heads)
    NCT = len(cts)
    qc = np.concatenate([np.arange(h * 256, (h + 1) * 256) for h in heads])
    vc = np.concatenate([np.arange(h * 512, (h + 1) * 512) for h in heads])
    cc = np.concatenate([np.arange(c * 128, (c + 1) * 128) for c in cts])
    O_Q, O_K, O_V, O_O, O_ZA, O_G = 0, 1024, 2048, 4096, 6144, 8192
    O_XB, O_ZB, O_GA, O_GB = 8208, 10256, 12304, 14352
    fm_cols = np.concatenate([O_Q + qc, O_K + qc, O_XB + cc, O_ZB + cc, O_GA + np.arange(D), O_GB + np.arange(D)])
    tm_cols = np.concatenate([O_K + qc, O_V + vc, O_O + vc, O_ZA + vc])
    hs = np.array(heads)
    g_cols = np.concatenate([O_G + hs, O_G + 8 + hs, O_G + 4 + hs, O_G + 12 + hs])
    b_idx = np.concatenate([hs, 8 + hs, 4 + hs, 12 + hs])
    w_in = inp["w_in"]
    d = {}
    d["x"] = np.ascontiguousarray(inp["x"][b, :S])
    d["wfm"] = np.ascontiguousarray(w_in[:depth][:, :, fm_cols])
    d["wtm"] = np.ascontiguousarray(w_in[:depth][:, :, tm_cols])
    wgc = w_in[:depth][:, :, g_cols]
    d["wg"] = np.ascontiguousarray(wgc.reshape(depth, 16, 128, -1).transpose(0, 2, 1, 3).reshape(depth, 128, -1))
    d["bg"] = np.ascontiguousarray(inp["b_if"][:depth][:, b_idx][:, :, None])
    d["normg"] = np.ascontiguousarray(inp["norm_g"][:depth])
    d["headg"] = np.ascontiguousarray(inp["head_g"][:depth][:, vc])
    cv = np.zeros((depth, NCT * 128, 12), np.float32)
    cv[:, :, 0:4] = np.transpose(inp["conv_w"][:depth][:, :, cc], (0, 2, 1))
    cv[:, :, 4] = inp["conv_b"][:depth][:, cc]
    brg = inp["b_rg"][:depth]
    cv[:, :, 5] = brg[:, 0, 0][:, cc]
    cv[:, :, 6] = brg[:, 0, 1][:, cc]
    cv[:, :, 7] = brg[:, 1, 0][:, cc]
    cv[:, :, 8] = brg[:, 1, 1][:, cc]
    cv[:, :, 9] = inp["lru_lambda"][:depth][:, 0][:, cc]
    cv[:, :, 10] = inp["lru_lambda"][:depth][:, 1][:, cc]
    cv[:, :, 11] = 1.0
    d["cvec"] = cv
    wr = inp["w_rg"][:depth]
    d["wrg"] = np.ascontiguousarray(wr[:, :, :, list(cts)].reshape(depth, 4, NCT, 128, 128))
    d["wa"] = np.ascontiguousarray(inp["w_branch_a"][:depth][:, vc, :])
    d["wb"] = np.ascontiguousarray(inp["w_branch_b"][:depth][:, cc, :])
    d["wout"] = np.ascontiguousarray(inp["w_out"][:depth])
    d["finalg"] = np.ascontiguousarray(inp["final_g"])
    ident, mask, sel = make_consts(HPC)
    d["c_ident"], d["c_mask"], d["c_sel"] = ident, mask, sel
    d["c_flag"] = np.ones((128, 1), np.float32)
    return d


_PROG = {}


def kernel(**inputs):
    inp = {k: np.asarray(v) for k, v in inputs.items()}
    B, S, _ = inp["x"].shape
    cfg = Cfg(S=S, HPC=2, NCT=8, DEPTH=2, split=2, groups=[[0, 1], [2, 3], [4, 5], [6, 7]])
    key = (S,)
    if key not in _PROG:
        _PROG[key] = build_program(cfg)
    nc = _PROG[key]
    halves = [prep_core(inp, 0, [2 * hh, 2 * hh + 1], list(range(8 * hh, 8 * hh + 8)), S, 2) for hh in range(2)]
    in_maps = []
    for core in range(2 * B):
        b, hh = core // 2, core % 2
        m = dict(halves[hh])
        m["x"] = np.ascontiguousarray(inp["x"][b, :S])
        m["c_flag"] = np.full((128, 1), 1.0 if hh == 0 else 0.0, np.float32)
        in_maps.append(m)
    res = run_bass_kernel_spmd(nc, in_maps, core_ids=list(range(2 * B)))
    outs = [np.asarray(res.results[2 * b]["out"]) for b in range(B)]
    return np.stack(outs, 0).astype(np.float32)
```

```python
import numpy as np
import concourse.bass as bass
import concourse.mybir as mybir
from concourse.bass_utils import run_bass_kernel_spmd

F32 = mybir.dt.float32
BF16 = mybir.dt.bfloat16
U8 = mybir.dt.uint8
AF = mybir.ActivationFunctionType
ALU = mybir.AluOpType
AX = mybir.AxisListType

D = 2048
L = 128
EPS = 1e-6
ENG = ["pe", "act", "dve", "pool", "sp"]


def _merge(d, s):
    for k, v in s.items():
        if d.get(k, 0) < v:
            d[k] = v


class Buf:
    __slots__ = ("name", "old", "w", "r", "sem")

    def __init__(self, name):
        self.name = name
        self.old = {}
        self.w = {}
        self.r = {}
        self.sem = None


class Sched:
    def __init__(self, nc, ndma):
        self.nc = nc
        self.h = {}
        for e in ENG:
            self.h[("e", e)] = nc.alloc_semaphore("es_" + e)
        for i in range(ndma):
            self.h[("d", i)] = nc.alloc_semaphore("ds%d" % i)
        self.ecnt = {e: 0 for e in ENG}
        self.dval = [0] * ndma
        self.dfree = list(range(ndma))
        self.dbufs = []
        self.seen = {e: {} for e in ENG}
        self.streams = {e: [] for e in ENG}

    def op(self, eng, fn, reads=(), writes=(), pwrites=(), dma=None, dinc=16):
        deps = {}
        for b in reads:
            _merge(deps, b.w)
        for b in writes:
            old = {}
            _merge(old, b.w)
            _merge(old, b.r)
            b.old = old
            b.w = {}
            b.r = {}
        for b in list(writes) + list(pwrites):
            _merge(deps, b.old)
            _merge(deps, b.r)
        if dma is not None:
            if dma.sem is None:
                dma.sem = self.dfree.pop()
                self.dbufs.append(dma)
            key = ("d", dma.sem)
            self.dval[dma.sem] += dinc
            val = self.dval[dma.sem]
            inc = dinc
        else:
            key = ("e", eng)
            self.ecnt[eng] += 1
            val = self.ecnt[eng]
            inc = 1
        waits = []
        seen = self.seen[eng]
        for k, v in deps.items():
            if k == ("e", "pe") and eng == "pe":
                continue
            if seen.get(k, 0) < v:
                seen[k] = v
                waits.append((k, v))
        self.streams[eng].append((waits, fn, key, inc))
        ev = {key: val}
        for b in reads:
            _merge(b.r, ev)
        for b in list(writes) + list(pwrites):
            _merge(b.w, ev)

    def newgen(self, b):
        old = {}
        _merge(old, b.w)
        _merge(old, b.r)
        b.old = old
        b.w = {}
        b.r = {}

    def barrier(self):
        allev = {("e", e): self.ecnt[e] for e in ENG if self.ecnt[e] > 0}
        for i, v in enumerate(self.dval):
            if v > 0:
                allev[("d", i)] = v
        for e in ENG:
            waits = []
            seen = self.seen[e]
            for k, v in allev.items():
                if k == ("e", e):
                    continue
                if seen.get(k, 0) < v:
                    seen[k] = v
                    waits.append((k, v))
            if waits:
                self.streams[e].append((waits, None, None, 0))
        for b in self.dbufs:
            b.sem = None
        self.dbufs = []
        self.dfree = list(range(len(self.dval)))

    def replay(self, e, name):
        for waits, fn, key, inc in self.streams[name]:
            for k, v in waits:
                e.wait_ge(self.h[k], v)
            if fn is not None:
                fn(e).then_inc(self.h[key], inc)


class Arena:
    def __init__(self, nc, nbytes):
        self.t = nc.alloc_sbuf_tensor("arena", [128, nbytes], U8)
        self.n = nbytes
        self.cur = 0
        self.base = 0

    def alloc(self, shape, dt):
        esz = 4 if dt == F32 else 2
        nb = int(np.prod(shape[1:])) * esz
        nb_al = (nb + 63) // 64 * 64
        assert self.cur + nb_al <= self.n, ("SBUF arena overflow", self.cur, nb_al, self.n)
        v = self.t[:, self.cur:self.cur + nb].bitcast(dt)
        self.cur += nb_al
        if len(shape) == 3:
            v = v.rearrange("p (a b) -> p a b", a=shape[1])
        elif len(shape) == 4:
            v = v.rearrange("p (a b c) -> p a b c", a=shape[1], b=shape[2])
        return v[0:shape[0]]

    def mark(self):
        self.base = self.cur

    def reset(self):
        self.cur = self.base


class Cfg:
    def __init__(self, S=4096, HPC=4, NCT=16, DEPTH=2, final=True, debug=(), stop=None, split=1, groups=None):
        self.stop = stop
        self.split = split
        self.groups = groups
        self.S, self.HPC, self.NCT, self.DEPTH, self.final = S, HPC, NCT, DEPTH, final
        self.NT = S // 128
        self.NC = S // L
        self.NB = S // 512
        self.QW = HPC * 256
        self.VW = HPC * 512
        self.CW = NCT * 128
        self.NFM = 2 * self.QW + 2 * self.CW + 2 * D
        self.NTM = self.QW + 3 * self.VW
        self.debug = debug


def build_program(cfg):
    S, HPC, NCT, NT, NC, NB = cfg.S, cfg.HPC, cfg.NCT, cfg.NT, cfg.NC, cfg.NB
    QW, VW, CW, NFM, NTM = cfg.QW, cfg.VW, cfg.CW, cfg.NFM, cfg.NTM
    G4 = 4 * HPC
    nc = bass.Bass("TRN2", target_bir_lowering=False)
    DEPTH = cfg.DEPTH

    def din(name, shape, dt=F32):
        return nc.dram_tensor(name, list(shape), dt, kind="ExternalInput").ap()

    def dscr(name, shape, dt):
        kind = "ExternalOutput" if name in cfg.debug else "Internal"
        return nc.dram_tensor(name, list(shape), dt, kind=kind).ap()

    x_in = din("x", [S, D])
    wfm = din("wfm", [DEPTH, D, NFM])
    wtm = din("wtm", [DEPTH, D, NTM])
    wg = din("wg", [DEPTH, 128, 16 * G4])
    bg = din("bg", [DEPTH, G4, 1])
    normg = din("normg", [DEPTH, D])
    headg = din("headg", [DEPTH, VW])
    cvec = din("cvec", [DEPTH, CW, 12])
    wrg = din("wrg", [DEPTH, 4, NCT, 128, 128])
    wa = din("wa", [DEPTH, VW, D])
    wb = din("wb", [DEPTH, CW, D])
    wout = din("wout", [DEPTH, D, D])
    finalg = din("finalg", [D])
    c_ident = din("c_ident", [128, 128])
    c_mask = din("c_mask", [L, 2, L])
    c_sel = din("c_sel", [HPC, HPC, 128])
    c_flag = din("c_flag", [128, 1])
    out = nc.dram_tensor("out", [S, D], F32, kind="ExternalOutput").ap()

    QT = dscr("QT", [QW, S], BF16)
    KT = dscr("KT", [QW, S], BF16)
    KK = dscr("KK", [S, QW], BF16)
    VE = dscr("VE", [2, S, VW], BF16)
    SO = dscr("SO", [S, VW], BF16)
    SZ = dscr("SZ", [S, VW], BF16)
    XB = dscr("XB", [CW, S], BF16)
    ZB = dscr("ZB", [CW, S], BF16)
    GA = dscr("GA", [D, S], BF16)
    GB = dscr("GB", [D, S], BF16)
    HF = dscr("HF", [S, VW], F32)
    YAT = dscr("YAT", [VW, S], BF16)
    YBT = dscr("YBT", [CW, S], BF16)
    MT = dscr("MT", [D, S], BF16)
    X1 = dscr("X1", [S, D], F32)
    HTS = dscr("HTS", [128, 16, S], BF16)
    PP = dscr("PP", [S, D], F32)
    PSL = [dscr("PS%d" % i, [S, D], F32) for i in range(DEPTH)]
    dbuf = {n: Buf(n) for n in ["QT", "KT", "KK", "VE", "SO", "SZ", "XB", "ZB", "GA", "GB", "HF",
                                "YAT", "YBT", "MT", "X1", "OUT", "IN", "PP", "HTS"] + ["PS%d" % i for i in range(DEPTH)]}

    A = Arena(nc, 206 * 1024)
    banks = [nc.alloc_psum_tensor("bank%d" % i, [128, 512], F32) for i in range(8)]
    bankb = [Buf("bank%d" % i) for i in range(8)]
    sc = Sched(nc, 84)
    IN = dbuf["IN"]

    identf = A.alloc([128, 128], F32)
    identb = A.alloc([128, 128], BF16)
    maskb = A.alloc([L, 2, L], BF16)
    self_ = A.alloc([HPC, HPC, 128], F32)
    b_const = Buf("const")
    flagt = A.alloc([128, 1], F32)
    sc.op("sp", lambda e: e.dma_start(out=identf, in_=c_ident[:, :]), reads=[IN], pwrites=[b_const], dma=b_const)
    sc.op("sp", lambda e: e.dma_start(out=self_, in_=c_sel[:, :, :]), reads=[IN], pwrites=[b_const], dma=b_const)
    sc.op("sp", lambda e: e.dma_start(out=flagt, in_=c_flag[:, :]), reads=[IN], pwrites=[b_const], dma=b_const)
    sc.op("pool", lambda e: e.dma_start(out=identb, in_=c_ident[:, :]), reads=[IN], pwrites=[b_const], dma=b_const)
    sc.op("pool", lambda e: e.dma_start(out=maskb, in_=c_mask[:, :, :]), reads=[IN], pwrites=[b_const], dma=b_const)
    ETK = A.alloc([128, NT, 2 * HPC], F32)
    ECH = A.alloc([L, NC, 4 * HPC], F32)
    ECHB = A.alloc([L, NC, 2 * HPC], BF16)
    DEC = A.alloc([128, 2 * HPC, NC], F32)
    b_tab = Buf("tables")
    A.mark()

    def dump(name, ap, b, shape, dt=F32):
        t = nc.dram_tensor("dbg_" + name, list(shape), dt, kind="ExternalOutput").ap()
        db = Buf("dbg_" + name)
        sc.op("sp", lambda e: e.dma_start(out=t, in_=ap), reads=[b], writes=[db], dma=db)

    def norm_tile(l, tt, xsrc, xsrc_buf, xt, b_xt, junk, b_junk, st, b_st, gbc, b_gbc, xn, b_xn, part=None):
        if part in (None, "A"):
            sc.op("sp", lambda e: e.dma_start(out=xt, in_=xsrc[tt * 128:(tt + 1) * 128, :]),
                  reads=[xsrc_buf], writes=[b_xt], dma=b_xt)
            sc.op("act", lambda e: e.activation(out=junk, in_=xt, func=AF.Square), reads=[b_xt], writes=[b_junk])
            sc.op("dve", lambda e: e.reduce_sum(out=st[:, 0:1], in_=junk, axis=AX.X), reads=[b_junk], writes=[b_st])
        if part in (None, "B"):
            sc.op("act", lambda e: e.activation(out=st[:, 1:2], in_=st[:, 0:1], func=AF.Sqrt, bias=st[:, 3:4], scale=1.0 / D),
                  reads=[b_st], pwrites=[b_st])
            sc.op("dve", lambda e: e.reciprocal(out=st[:, 2:3], in_=st[:, 1:2]), reads=[b_st], pwrites=[b_st])
            sc.op("dve", lambda e: e.scalar_tensor_tensor(out=xn, in0=xt, scalar=st[:, 2:3], in1=gbc,
                                                          op0=ALU.mult, op1=ALU.mult),
                  reads=[b_xt, b_st, b_gbc], writes=[b_xn])

    def layer(l, xsrc, xsrc_buf, last):
        A.reset()
        gbc = A.alloc([128, D], F32); b_gbc = Buf("gbc")
        sc.op("sp", lambda e: e.dma_start(out=gbc, in_=normg[l].partition_broadcast(128)),
              reads=[IN], writes=[b_gbc], dma=b_gbc)
        wgb = A.alloc([128, 16, G4], BF16); b_wgb = Buf("wgb")
        wgf = A.alloc([128, 16 * G4], F32); b_wgf = Buf("wgf")
        sc.op("sp", lambda e: e.dma_start(out=wgf, in_=wg[l]), reads=[IN], writes=[b_wgf], dma=b_wgf)
        sc.op("dve", lambda e: e.tensor_copy(out=wgb, in_=wgf.rearrange("p (a b) -> p a b", a=16)),
              reads=[b_wgf], writes=[b_wgb])
        bgt = A.alloc([G4, 1], F32); b_bgt = Buf("bgt")
        sc.op("sp", lambda e: e.dma_start(out=bgt, in_=bg[l]), reads=[IN], writes=[b_bgt], dma=b_bgt)
        xt = [A.alloc([128, D], F32) for _ in range(2)]; b_xt = [Buf("xt%d" % i) for i in range(2)]
        xn = [A.alloc([128, D], BF16) for _ in range(2)]; b_xn = [Buf("xn%d" % i) for i in range(2)]
        junk = A.alloc([128, D], BF16); b_junk = Buf("junk")
        st = [A.alloc([128, 4], F32) for _ in range(2)]; b_st = [Buf("st%d" % i) for i in range(2)]
        for i in range(2):
            sc.op("pool", lambda e, i=i: e.memset(st[i][:, 3:4], EPS), writes=[b_st[i]])
        hTs = [A.alloc([128, 16, 128], BF16) for _ in range(2)]; b_hTs = [Buf("hTs%d" % i) for i in range(2)]
        GT = A.alloc([G4, S], F32); b_GT = Buf("GT")
        sc.newgen(b_GT)
        sc.newgen(dbuf["HTS"])

        def transposes(i, b_dst_list, dst_fn):
            for half in range(2):
                pb = banks[6 + half][:].bitcast(BF16)
                for j in range(8):
                    jj = half * 8 + j
                    sc.op("pe", lambda e, jj=jj, j=j, pb=pb: e.transpose(out=pb[:, j * 128:(j + 1) * 128],
                                                                        in_=xn[i][:, jj * 128:(jj + 1) * 128],
                                                                        identity=identb),
                          reads=[b_xn[i], b_const], writes=[bankb[6 + half]] if j == 0 else (),
                          pwrites=() if j == 0 else [bankb[6 + half]])
                eng = "act" if half == 0 else "dve"
                dst = dst_fn(half)
                src = pb.rearrange("p (a b) -> p a b", a=8)
                if eng == "act":
                    sc.op("act", lambda e, dst=dst, src=src: e.activation(out=dst, in_=src, func=AF.Copy),
                          reads=[bankb[6 + half]], pwrites=b_dst_list)
                else:
                    sc.op("dve", lambda e, dst=dst, src=src: e.tensor_copy(out=dst, in_=src),
                          reads=[bankb[6 + half]], pwrites=b_dst_list)

        norm_tile(l, 0, xsrc, xsrc_buf, xt[0], b_xt[0], junk, b_junk, st[0], b_st[0], gbc, b_gbc, xn[0], b_xn[0])
        for tt in range(NT):
            i = tt % 2
            if tt + 1 < NT:
                i1 = (tt + 1) % 2
                norm_tile(l, tt + 1, xsrc, xsrc_buf, xt[i1], b_xt[i1], junk, b_junk, st[i1], b_st[i1], gbc, b_gbc, xn[i1], b_xn[i1], part="A")
            sc.newgen(b_hTs[i])
            transposes(i, [b_hTs[i]], lambda half, i=i: hTs[i][:, half * 8:(half + 1) * 8, :])
            sc.op("pool", lambda e, i=i, tt=tt: e.dma_start(out=HTS[:, :, tt * 128:(tt + 1) * 128], in_=hTs[i]),
                  reads=[b_hTs[i]], pwrites=[dbuf["HTS"]], dma=b_hTs[i])
            bk = tt % 2
            for j in range(16):
                sc.op("pe", lambda e, j=j, i=i, bk=bk: e.matmul(banks[bk][0:G4, 0:128], lhsT=wgb[:, j, :], rhs=hTs[i][:, j, :],
                                                              start=(j == 0), stop=(j == 15)),
                      reads=[b_wgb, b_hTs[i]], writes=[bankb[bk]] if j == 0 else (), pwrites=() if j == 0 else [bankb[bk]])
            sc.op("act", lambda e, tt=tt, bk=bk: e.activation(out=GT[:, tt * 128:(tt + 1) * 128], in_=banks[bk][0:G4, 0:128],
                                                            func=AF.Identity, bias=bgt[:, 0:1]),
                  reads=[bankb[bk], b_bgt], pwrites=[b_GT])
            if tt + 1 < NT:
                i1 = (tt + 1) % 2
                norm_tile(l, tt + 1, xsrc, xsrc_buf, xt[i1], b_xt[i1], junk, b_junk, st[i1], b_st[i1], gbc, b_gbc, xn[i1], b_xn[i1], part="B")
        if cfg.stop == "PG0":
            dump("GT", GT, b_GT, [G4, S])
            return
        gi = A.alloc([HPC, S], F32); gf = A.alloc([HPC, S], F32)
        t1 = A.alloc([HPC, S], F32); t2 = A.alloc([HPC, S], F32)
        b_gi, b_gf, b_t1, b_t2 = Buf("gi"), Buf("gf"), Buf("t1"), Buf("t2")
        mpc = A.alloc([HPC, NC], F32); dd = A.alloc([HPC, NC], F32)
        b_mpc, b_dd = Buf("mpc"), Buf("dd")
        pst = banks[0]; b_pst = bankb[0]
        sc.newgen(b_tab)
        for d in range(2):
            rv = (lambda ap: ap) if d == 0 else (lambda ap: ap[:, ::-1])
            sc.op("sp", lambda e, d=d: e.dma_start(out=gi, in_=GT[(2 * d) * HPC:(2 * d + 1) * HPC, :]),
                  reads=[b_GT], writes=[b_gi], dma=b_gi)
            sc.op("sp", lambda e, d=d: e.dma_start(out=gf, in_=GT[(2 * d + 1) * HPC:(2 * d + 2) * HPC, :]),
                  reads=[b_GT], writes=[b_gf], dma=b_gf)
            sc.op("act", lambda e: e.activation(out=t1, in_=gf, func=AF.Exp, scale=-1.0), reads=[b_gf], writes=[b_t1])
            sc.op("act", lambda e: e.activation(out=t2, in_=t1, func=AF.Ln, bias=1.0, scale=1.0), reads=[b_t1], writes=[b_t2])
            sc.op("dve", lambda e, rv=rv: e.tensor_tensor_scan(out=rv(t1), data0=rv(t2), data1=rv(t2), initial=0.0,
                                                             op0=ALU.add, op1=ALU.max), reads=[b_t2], writes=[b_t1])
            sc.op("dve", lambda e: e.tensor_tensor(out=gf, in0=gi, in1=t1, op=ALU.add), reads=[b_gi, b_t1], writes=[b_gf])
            sc.op("dve", lambda e, rv=rv: e.tensor_tensor_scan(out=rv(t2), data0=rv(gf), data1=rv(gf), initial=0.0,
                                                             op0=ALU.max, op1=ALU.max), reads=[b_gf], writes=[b_t2])
            mm3 = t2.rearrange("p (c l) -> p c l", l=L)
            sc.op("dve", lambda e: e.memset(mpc, 0.0), writes=[b_mpc])
            if NC > 1:
                if d == 0:
                    sc.op("dve", lambda e, mm3=mm3: e.tensor_copy(out=mpc[:, 1:NC], in_=mm3[:, 0:NC - 1, L - 1]),
                          reads=[b_t2], pwrites=[b_mpc])
                else:
                    sc.op("dve", lambda e, mm3=mm3: e.tensor_copy(out=mpc[:, 0:NC - 1], in_=mm3[:, 1:NC, 0]),
                          reads=[b_t2], pwrites=[b_mpc])
            sc.op("dve", lambda e: e.memset(dd, 0.0), writes=[b_dd])
            if NC > 1:
                if d == 0:
                    sc.op("dve", lambda e: e.tensor_tensor(out=dd[:, 1:NC], in0=mpc[:, 0:NC - 1], in1=mpc[:, 1:NC], op=ALU.subtract),
                          reads=[b_mpc], pwrites=[b_dd])
                else:
                    sc.op("dve", lambda e: e.tensor_tensor(out=dd[:, 0:NC - 1], in0=mpc[:, 1:NC], in1=mpc[:, 0:NC - 1], op=ALU.subtract),
                          reads=[b_mpc], pwrites=[b_dd])
            sc.op("act", lambda e: e.activation(out=dd, in_=dd, func=AF.Exp), reads=[b_dd], writes=[b_dd])
            mpb = mpc.unsqueeze(2).to_broadcast([HPC, NC, L])
            a3 = gf.rearrange("p (c l) -> p c l", l=L)
            b3 = t1.rearrange("p (c l) -> p c l", l=L)
            sc.op("dve", lambda e, a3=a3, mpb=mpb: e.tensor_tensor(out=a3, in0=a3, in1=mpb, op=ALU.subtract),
                  reads=[b_mpc, b_gf], writes=[b_gf])
            sc.op("dve", lambda e, b3=b3, mpb=mpb: e.tensor_tensor(out=b3, in0=b3, in1=mpb, op=ALU.subtract),
                  reads=[b_mpc, b_t1], writes=[b_t1])
            sc.op("act", lambda e: e.activation(out=gf, in_=gf, func=AF.Exp), reads=[b_gf], writes=[b_gf])
            sc.op("act", lambda e: e.activation(out=t1, in_=t1, func=AF.Exp), reads=[b_t1], writes=[b_t1])
            idh = identf[0:HPC, 0:HPC]
            for t0 in range(0, NT, 64):
                tn = min(64, NT - t0)
                for k in range(tn):
                    tt = t0 + k
                    sc.op("pe", lambda e, tt=tt, k=k: e.transpose(out=pst[:, k * HPC:(k + 1) * HPC],
                                                                  in_=gf[:, tt * 128:(tt + 1) * 128], identity=idh),
                          reads=[b_gf, b_const], writes=[b_pst] if k == 0 else (), pwrites=() if k == 0 else [b_pst])
                sc.op("dve", lambda e, t0=t0, tn=tn, d=d: e.tensor_copy(
                    out=ETK[:, t0:t0 + tn, d * HPC:(d + 1) * HPC],
                    in_=pst[:, 0:tn * HPC].rearrange("p (a b) -> p a b", b=HPC)), reads=[b_pst], pwrites=[b_tab])
            for q, (srcap, b_src) in enumerate([(gf, b_gf), (t1, b_t1)]):
                for c0 in range(0, NC, 64):
                    cn = min(64, NC - c0)
                    for k in range(cn):
                        c = c0 + k
                        sc.op("pe", lambda e, c=c, k=k, srcap=srcap: e.transpose(out=pst[0:L, k * HPC:(k + 1) * HPC],
                                                                                in_=srcap[:, c * L:(c + 1) * L], identity=idh),
                              reads=[b_src, b_const], writes=[b_pst] if k == 0 else (), pwrites=() if k == 0 else [b_pst])
                    col = (2 * q + d) * HPC
                    sc.op("dve", lambda e, c0=c0, cn=cn, col=col: e.tensor_copy(
                        out=ECH[:, c0:c0 + cn, col:col + HPC],
                        in_=pst[0:L, 0:cn * HPC].rearrange("p (a b) -> p a b", b=HPC)), reads=[b_pst], pwrites=[b_tab])
            for h in range(HPC):
                sc.op("pe", lambda e, h=h: e.matmul(pst[:, 0:NC], lhsT=self_[:, h, :], rhs=dd, start=True, stop=True),
                      reads=[b_dd, b_const], writes=[b_pst])
                sc.op("dve", lambda e, h=h, d=d: e.tensor_copy(out=DEC[:, d * HPC + h, :], in_=pst[:, 0:NC]),
                      reads=[b_pst], pwrites=[b_tab])
        sc.op("dve", lambda e: e.tensor_copy(out=ECHB, in_=ECH[:, :, 0:2 * HPC]), reads=[b_tab], pwrites=[b_tab])
        if "ETK1" in cfg.debug:
            dump("GT1", GT, b_GT, [G4, S])
            dump("ETK1", ETK, b_tab, [128, NT, 2 * HPC])
        if cfg.stop == "PG":
            dump("GT", GT, b_GT, [G4, S])
            dump("ETK", ETK, b_tab, [128, NT, 2 * HPC])
            dump("ECH", ECH, b_tab, [L, NC, 4 * HPC])
            dump("DEC", DEC, b_tab, [128, 2 * HPC, NC])
            return
        sc.barrier()

        A.reset()
        gbc_p1 = A.alloc([128, D], F32); b_gbc_p1 = Buf("gbc")
        sc.op("sp", lambda e: e.dma_start(out=gbc_p1, in_=normg[l].partition_broadcast(128)),
              reads=[IN], writes=[b_gbc_p1], dma=b_gbc_p1)
        hT = A.alloc([128, 16, S], BF16)
        b_hT = [Buf("hT%d" % t) for t in range(NT)]
        b_hTld = [Buf("hTld%d" % t) for t in range(NB)]
        wblk = [A.alloc([128, 16, 512], BF16) for _ in range(2)]; b_wblk = [Buf("wblk%d" % i) for i in range(2)]
        NSTG = 4
        stg = [A.alloc([128, 512], BF16) for _ in range(NSTG)]; b_stg = [Buf("stg%d" % i) for i in range(NSTG)]
        stg2 = [A.alloc([128, 512], BF16) for _ in range(2)]; b_stg2 = [Buf("stgb%d" % i) for i in range(2)]
        sgt = [A.alloc([128, 512], F32) for _ in range(2)]; b_sgt = [Buf("sgt%d" % i) for i in range(2)]
        hgp = A.alloc([128, 512], F32); b_hgp = Buf("hgp")
        _xt1 = A.alloc([128, D], F32); _bxt1 = Buf("xt")
        xt_p1 = [_xt1, _xt1]; b_xt_p1 = [_bxt1, _bxt1]
        _xn1 = A.alloc([128, D], BF16); _bxn1 = Buf("xn")
        xn_p1 = [_xn1, _xn1]; b_xn_p1 = [_bxn1, _bxn1]
        junk_p1 = wblk[1][:, 0:4, :].rearrange("p a b -> p (a b)"); b_junk_p1 = b_wblk[1]
        st_p1 = [A.alloc([128, 4], F32) for _ in range(2)]; b_st_p1 = [Buf("st%d" % i) for i in range(2)]
        for i in range(2):
            sc.op("pool", lambda e, i=i: e.memset(st_p1[i][:, 3:4], EPS), writes=[b_st_p1[i]])

        def transposes1(i, tt):
            for half in range(2):
                pb = banks[6 + half][:].bitcast(BF16)
                for j in range(8):
                    jj = half * 8 + j
                    sc.op("pe", lambda e, jj=jj, j=j, pb=pb: e.transpose(out=pb[:, j * 128:(j + 1) * 128],
                                                                        in_=xn_p1[i][:, jj * 128:(jj + 1) * 128],
                                                                        identity=identb),
                          reads=[b_xn_p1[i], b_const], writes=[bankb[6 + half]] if j == 0 else (),
                          pwrites=() if j == 0 else [bankb[6 + half]])
                dst = hT[:, half * 8:(half + 1) * 8, tt * 128:(tt + 1) * 128]
                src = pb.rearrange("p (a b) -> p a b", a=8)
                if half == 0:
                    sc.op("act", lambda e, dst=dst, src=src: e.activation(out=dst, in_=src, func=AF.Copy),
                          reads=[bankb[6 + half]], pwrites=[b_hT[tt]])
                else:
                    sc.op("dve", lambda e, dst=dst, src=src: e.tensor_copy(out=dst, in_=src),
                          reads=[bankb[6 + half]], pwrites=[b_hT[tt]])

        blocks = []
        for i in range(QW // 512):
            blocks.append(("fm", wfm[l][:, i * 512:(i + 1) * 512], ("copy", QT, "QT", i * 512, 1.0)))
        for i in range(QW // 512):
            blocks.append(("fm", wfm[l][:, QW + i * 512:QW + (i + 1) * 512], ("copy", KT, "KT", i * 512, 1.0 / 16.0)))
        for i in range(QW // 512):
            blocks.append(("tm", wtm[l][:, i * 512:(i + 1) * 512], ("copy", KK, "KK", i * 512, 1.0 / 16.0)))
        for i in range(VW // 512):
            blocks.append(("tm", wtm[l][:, QW + i * 512:QW + (i + 1) * 512], ("vscale", None, "VE", i, 1.0)))
        for i in range(VW // 512):
            blocks.append(("tm", wtm[l][:, QW + VW + i * 512:QW + VW + (i + 1) * 512], ("sighg", SO, "SO", i * 512, 1.0)))
        for i in range(VW // 512):
            blocks.append(("tm", wtm[l][:, QW + 2 * VW + i * 512:QW + 2 * VW + (i + 1) * 512], ("silu", SZ, "SZ", i * 512, 1.0)))
        o0 = 2 * QW
        for i in range(CW // 512):
            blocks.append(("fm", wfm[l][:, o0 + i * 512:o0 + (i + 1) * 512], ("copy", XB, "XB", i * 512, 1.0)))
        o0 += CW
        for i in range(CW // 512):
            blocks.append(("fm", wfm[l][:, o0 + i * 512:o0 + (i + 1) * 512], ("silu", ZB, "ZB", i * 512, 1.0)))
        o0 += CW
        for i in range(D // 512):
            blocks.append(("fm", wfm[l][:, o0 + i * 512:o0 + (i + 1) * 512], ("sigmoid", GA, "GA", i * 512, 1.0)))
        o0 += D
        for i in range(D // 512):
            blocks.append(("fm", wfm[l][:, o0 + i * 512:o0 + (i + 1) * 512], ("sigmoid", GB, "GB", i * 512, 1.0)))

        def load_w(bi):
            kind, wsrc, spec = blocks[bi]
            w = bi % 2
            for hh in range(16):
                sc.op("pool", lambda e, w=w, wsrc=wsrc, hh=hh: e.dma_start(
                    out=wblk[w][:, hh, :], in_=wsrc[hh * 128:(hh + 1) * 128, :]),
                    reads=[IN], writes=[b_wblk[w]] if hh == 0 else (), pwrites=() if hh == 0 else [b_wblk[w]],
                    dma=b_wblk[w])

        ucount = [0]

        def evac(spec, ps, b_ps, r0, c0, tm, tt):
            fn, dst, dname, off, scale = spec
            u = ucount[0]; ucount[0] += 1
            if fn == "vscale":
                hd = off
                for d in range(2):
                    s2 = u % 2 if d == 0 else (u + 1) % 2
                    sb = stg2[d]; b_sb = b_stg2[d]
                    col = d * HPC + hd
                    if False:
                        sc.op("act", lambda e, sb=sb, col=col: e.activation(out=sb, in_=ps, func=AF.Identity, scale=ETK[:, tt, col:col + 1]),
                              reads=[b_ps, b_tab], writes=[b_sb])
                    else:
                        sc.op("dve", lambda e, sb=sb, col=col: e.tensor_scalar(out=sb, in0=ps, scalar1=ETK[:, tt, col:col + 1], scalar2=None,
                                                                              op0=ALU.mult), reads=[b_ps, b_tab], writes=[b_sb])
                    sc.op("sp", lambda e, sb=sb, d=d, hd=hd: e.dma_start(out=VE[d, r0:r0 + 128, hd * 512:(hd + 1) * 512], in_=sb),
                          reads=[b_sb], pwrites=[dbuf["VE"]], dma=b_sb)
                return
            k = u % NSTG
            sb = stg[k]; b_sb = b_stg[k]
            if fn == "copy":
                if u % 2 == 0:
                    sc.op("act", lambda e: e.activation(out=sb, in_=ps, func=AF.Identity, scale=scale), reads=[b_ps], writes=[b_sb])
                else:
                    sc.op("dve", lambda e: e.tensor_scalar(out=sb, in0=ps, scalar1=scale, scalar2=None, op0=ALU.mult),
                          reads=[b_ps], writes=[b_sb])
            elif fn == "sigmoid":
                sc.op("act", lambda e: e.activation(out=sb, in_=ps, func=AF.Sigmoid), reads=[b_ps], writes=[b_sb])
            elif fn == "sighg":
                g = sgt[u % 2]; b_g = b_sgt[u % 2]
                sc.op("act", lambda e: e.activation(out=g, in_=ps, func=AF.Sigmoid), reads=[b_ps], writes=[b_g])
                sc.op("dve", lambda e: e.tensor_tensor(out=sb, in0=g, in1=hgp, op=ALU.mult), reads=[b_g, b_hgp], writes=[b_sb])
            elif fn == "silu":
                g = sgt[u % 2]; b_g = b_sgt[u % 2]
                sc.op("act", lambda e: e.activation(out=g, in_=ps, func=AF.Sigmoid), reads=[b_ps], writes=[b_g])
                sc.op("dve", lambda e: e.tensor_tensor(out=sb, in0=ps, in1=g, op=ALU.mult), reads=[b_ps, b_g], writes=[b_sb])
            sc.op("sp", lambda e: e.dma_start(out=dst[r0:r0 + 128, c0:c0 + 512], in_=sb),
                  reads=[b_sb], pwrites=[dbuf[dname]], dma=b_sb)

        pcount = [0]

        def run_block(bi, tts=None):
            kind, wsrc, spec = blocks[bi]
            w = bi % 2
            if spec[0] == "sighg":
                c0_ = spec[3]
                sc.op("sp", lambda e: e.dma_start(out=hgp, in_=headg[l][c0_:c0_ + 512].partition_broadcast(128)),
                      reads=[IN], writes=[b_hgp], dma=b_hgp)
            if kind == "tm":
                for tt in range(NT):
                    bk = pcount[0] % 6; pcount[0] += 1
                    for j in range(16):
                        sc.op("pe", lambda e, j=j, tt=tt, bk=bk: e.matmul(banks[bk][:, :], lhsT=hT[:, j, tt * 128:(tt + 1) * 128],
                                                                        rhs=wblk[w][:, j, :], start=(j == 0), stop=(j == 15)),
                              reads=[b_hT[tt], b_wblk[w]], writes=[bankb[bk]] if j == 0 else (),
                              pwrites=() if j == 0 else [bankb[bk]])
                    off = spec[3]
                    evac(spec, banks[bk][:, :], bankb[bk], tt * 128, off, True, tt)
            else:
                for ct in range(4):
                    for tb in range(NB):
                        bk = pcount[0] % 6; pcount[0] += 1
                        for j in range(16):
                            sc.op("pe", lambda e, j=j, ct=ct, tb=tb, bk=bk: e.matmul(
                                banks[bk][:, :], lhsT=wblk[w][:, j, ct * 128:(ct + 1) * 128],
                                rhs=hT[:, j, tb * 512:(tb + 1) * 512], start=(j == 0), stop=(j == 15)),
                                reads=[b_hT[t] for t in range(tb * 4, tb * 4 + 4)] + [b_wblk[w]],
                                writes=[bankb[bk]] if j == 0 else (), pwrites=() if j == 0 else [bankb[bk]])
                        off = spec[3]
                        evac(spec, banks[bk][:, :], bankb[bk], off + ct * 128, tb * 512, False, None)

        for dn in ["QT", "KT", "KK", "VE", "SO", "SZ", "XB", "ZB", "GA", "GB"]:
            sc.newgen(dbuf[dn])
        load_w(0)
        for tb in range(NB):
            for q in range(4):
                sc.newgen(b_hT[tb * 4 + q])
            bl = [b_hT[tb * 4 + q] for q in range(4)]
            for half in range(2):
                sc.op("sp", lambda e, tb=tb, half=half: e.dma_start(out=hT[:, half * 8:(half + 1) * 8, tb * 512:(tb + 1) * 512],
                                                                  in_=HTS[:, half * 8:(half + 1) * 8, tb * 512:(tb + 1) * 512]),
                      reads=[dbuf["HTS"]], pwrites=bl, dma=b_hTld[tb])
        if cfg.stop == "P1a":
            dump("hT", hT[:, 0, :], b_hT[NT - 1], [128, S], BF16)
            return
        nblk = len(blocks)
        if cfg.stop is not None and cfg.stop.startswith("P1b"):
            nblk = int(cfg.stop[3:])
        for bi in range(nblk):
            if bi + 1 < nblk:
                load_w(bi + 1)
            run_block(bi)
        if cfg.stop is not None:
            return
        if "ETK2" in cfg.debug:
            dump("ETK2", ETK, b_tab, [128, NT, 2 * HPC])
        sc.barrier()
        if cfg.stop == "P1":
            return
        phase2(l)
        sc.barrier()
        if cfg.stop == "P2":
            return
        phase3(l)
        sc.barrier()
        if cfg.stop == "P3":
            return
        phase4a(l)
        sc.barrier()
        if cfg.stop == "P4a":
            return
        phase4b(l, xsrc, xsrc_buf, last)
        sc.barrier()
        return


    def phase2(l):
        A.reset()
        GC = max(1, 512 // L)
        while NC % GC:
            GC //= 2
        NG = NC // GC
        TG = GC * L
        epsc = A.alloc([L, 1], F32); b_epsc = Buf("epsc")
        sc.op("pool", lambda e: e.memset(epsc, EPS), writes=[b_epsc])

        def mk(shape, dt, name):
            return A.alloc(shape, dt), Buf(name)

        hl_bufs = []
        for hl in range(2):
            Bf = {}
            for s_ in range(2):
                Bf[("qT", s_)] = mk([128, 2, TG], BF16, "qT")
                Bf[("kT", s_)] = mk([128, 2, TG], BF16, "kT")
                Bf[("kk", s_)] = mk([L, GC, 256], BF16, "kk")
                Bf[("ve", s_)] = mk([L, GC, 512], BF16, "ve")
                Bf[("hf", s_)] = mk([L, GC, 512], F32, "hf")
                Bf[("so", s_)] = mk([L, GC, 512], BF16, "so")
                Bf[("sz", s_)] = mk([L, GC, 512], BF16, "sz")
                Bf[("qs", s_)] = mk([128, 2, TG], BF16, "qs")
                Bf[("t3", s_)] = mk([L, 512], F32, "t3")
                Bf[("yT", s_)] = mk([128, 4, TG], BF16, "yT")
                Bf[("swm", s_)] = mk([L, L], BF16, "swm")
                Bf[("hs", s_)] = mk([L, 512], F32, "hs")
                Bf[("ya", s_)] = mk([L, 512], BF16, "ya")
                Bf[("dm", s_)] = mk([L, 2], F32, "dm")
                Bf[("sq", s_)] = mk([L, 4], F32, "sq")
            Bf["U"] = mk([128, 2, 512], F32, "U")
            Bf["Un"] = mk([128, 2], F32, "Un")
            Bf["Cbf"] = mk([128, 2, 512], BF16, "Cbf")
            Bf["nbf"] = mk([128, 2], BF16, "nbf")
            Bf["junk"] = mk([L, 512], BF16, "junk")
            bk = hl * 4
            Bf["ps_s"] = (banks[bk][0:L, 0:L], Buf("ps_s"))
            Bf["ps_den"] = (banks[bk][0:L, L:L + 1], Buf("ps_den"))
            Bf["ps_dn"] = (banks[bk][:, L + 2:L + 4], Buf("ps_dn"))
            Bf["ps_tp"] = (banks[bk][:].bitcast(BF16)[:, 512:512 + 4 * L], Buf("ps_tp"))
            Bf["ps_n"] = (banks[bk + 1][0:L, :], bankb[bk + 1])
            Bf["ps_c0"] = (banks[bk + 2][:, :], bankb[bk + 2])
            Bf["ps_c1"] = (banks[bk + 3][:, :], bankb[bk + 3])
            hl_bufs.append(Bf)

        def load_group(hl, hg, d, g, slot):
            Bf = hl_bufs[hl]
            t0 = g * TG
            qT, b_qT = Bf[("qT", slot)]; kT, b_kT = Bf[("kT", slot)]
            kk, b_kk = Bf[("kk", slot)]; ve, b_ve = Bf[("ve", slot)]
            sc.op("sp", lambda e: e.dma_start(out=qT, in_=QT[hg * 256:(hg + 1) * 256, t0:t0 + TG].rearrange("(k p) t -> p k t", p=128)),
                  reads=[dbuf["QT"]], writes=[b_qT], dma=b_qT)
            sc.op("sp", lambda e: e.dma_start(out=kT, in_=KT[hg * 256:(hg + 1) * 256, t0:t0 + TG].rearrange("(k p) t -> p k t", p=128)),
                  reads=[dbuf["KT"]], writes=[b_kT], dma=b_kT)
            sc.op("sp", lambda e: e.dma_start(out=kk, in_=KK[t0:t0 + TG, hg * 256:(hg + 1) * 256].rearrange("(c p) f -> p c f", p=L)),
                  reads=[dbuf["KK"]], writes=[b_kk], dma=b_kk)
            sc.op("sp", lambda e: e.dma_start(out=ve, in_=VE[d, t0:t0 + TG, hg * 512:(hg + 1) * 512].rearrange("(c p) f -> p c f", p=L)),
                  reads=[dbuf["VE"]], writes=[b_ve], dma=b_ve)
            if d == 1:
                hf, b_hf = Bf[("hf", slot)]; so, b_so = Bf[("so", slot)]; sz, b_sz = Bf[("sz", slot)]
                sc.op("sp", lambda e: e.dma_start(out=hf, in_=HF[t0:t0 + TG, hg * 512:(hg + 1) * 512].rearrange("(c p) f -> p c f", p=L)),
                      reads=[dbuf["HF"]], writes=[b_hf], dma=b_hf)
                sc.op("sp", lambda e: e.dma_start(out=so, in_=SO[t0:t0 + TG, hg * 512:(hg + 1) * 512].rearrange("(c p) f -> p c f", p=L)),
                      reads=[dbuf["SO"]], writes=[b_so], dma=b_so)
                sc.op("sp", lambda e: e.dma_start(out=sz, in_=SZ[t0:t0 + TG, hg * 512:(hg + 1) * 512].rearrange("(c p) f -> p c f", p=L)),
                      reads=[dbuf["SZ"]], writes=[b_sz], dma=b_sz)
            qs, b_qs = Bf[("qs", slot)]
            dinb = DEC[:, d * HPC + hg, g * GC:(g + 1) * GC].unsqueeze(2).to_broadcast([128, GC, L])
            for kt in range(2):
                sc.op("dve", lambda e, kt=kt: e.tensor_tensor(out=qs[:, kt, :].rearrange("p (c t) -> p c t", c=GC),
                                                              in0=qT[:, kt, :].rearrange("p (c t) -> p c t", c=GC), in1=dinb, op=ALU.mult),
                      reads=[b_qT, b_tab], writes=[b_qs] if kt == 0 else (), pwrites=() if kt == 0 else [b_qs])

        def chunk(hl, hg, d, c, slot, ci, stage):
            Bf = hl_bufs[hl]
            qT, b_qT = Bf[("qT", slot)]; kT, b_kT = Bf[("kT", slot)]
            kk, b_kk = Bf[("kk", slot)]; ve, b_ve = Bf[("ve", slot)]
            qs, b_qs = Bf[("qs", slot)]
            U, b_U = Bf["U"]; Un, b_Un = Bf["Un"]; Cbf, b_Cbf = Bf["Cbf"]; nbf, b_nbf = Bf["nbf"]
            ps_s, b_ps_s = Bf["ps_s"]; ps_den, b_ps_den = Bf["ps_den"]; ps_dn, b_ps_dn = Bf["ps_dn"]
            ps_n, b_ps_n = Bf["ps_n"]
            ps_c = [Bf["ps_c0"], Bf["ps_c1"]]
            p2 = c % 2
            swm, b_swm = Bf[("swm", p2)]; dm, b_dm = Bf[("dm", p2)]
            cs = slice(ci * L, (ci + 1) * L)
            din_ = DEC[:, d * HPC + hg, c:c + 1]
            ecol = ECHB[:, c, d * HPC + hg:d * HPC + hg + 1]
            thr = ECH[:, c, (2 + d) * HPC + hg:(2 + d) * HPC + hg + 1]
            if stage == "A":
                for kt in range(2):
                    sc.op("pe", lambda e, kt=kt: e.matmul(ps_s, lhsT=kT[:, kt, cs], rhs=qT[:, kt, cs], start=(kt == 0), stop=(kt == 1)),
                          reads=[b_kT, b_qT], writes=[b_ps_s] if kt == 0 else (), pwrites=() if kt == 0 else [b_ps_s])
                sc.op("dve", lambda e: e.tensor_tensor(out=swm, in0=ps_s, in1=maskb[:, d, :], op=ALU.mult),
                      reads=[b_ps_s, b_const], writes=[b_swm])
                return
            if stage == "B":
                sc.op("act", lambda e: e.activation(out=Cbf, in_=U, func=AF.Copy), reads=[b_U], writes=[b_Cbf])
                sc.op("act", lambda e: e.activation(out=nbf, in_=Un, func=AF.Copy), reads=[b_Un], writes=[b_nbf])
                return
            if stage == "C":
                for kt in range(2):
                    pc, b_pc = ps_c[kt]
                    sc.op("pe", lambda e, kt=kt, pc=pc: e.matmul(pc, lhsT=kk[:, ci, kt * 128:(kt + 1) * 128], rhs=ve[:, ci, :], start=True, stop=True),
                          reads=[b_kk, b_ve], writes=[b_pc])
                for kt in range(2):
                    sc.op("pe", lambda e, kt=kt: e.matmul(ps_dn[:, kt:kt + 1], lhsT=kk[:, ci, kt * 128:(kt + 1) * 128], rhs=ecol, start=True, stop=True),
                          reads=[b_kk, b_tab], writes=[b_ps_dn] if kt == 0 else (), pwrites=() if kt == 0 else [b_ps_dn])
                sc.op("pe", lambda e: e.matmul(ps_n, lhsT=swm, rhs=ve[:, ci, :], start=True, stop=False),
                      reads=[b_swm, b_ve], writes=[b_ps_n])
                for kt in range(2):
                    sc.op("pe", lambda e, kt=kt: e.matmul(ps_n, lhsT=qs[:, kt, cs], rhs=Cbf[:, kt, :], start=False, stop=(kt == 1)),
                          reads=[b_qs, b_Cbf], pwrites=[b_ps_n])
                sc.op("pe", lambda e: e.matmul(ps_den, lhsT=swm, rhs=ecol, start=True, stop=False),
                      reads=[b_swm, b_tab], writes=[b_ps_den])
                for kt in range(2):
                    sc.op("pe", lambda e, kt=kt: e.matmul(ps_den, lhsT=qs[:, kt, cs], rhs=nbf[:, kt:kt + 1], start=False, stop=(kt == 1)),
                          reads=[b_qs, b_nbf], pwrites=[b_ps_den])
                return
            for kt in range(2):
                pc, b_pc = ps_c[kt]
                sc.op("dve", lambda e, kt=kt, pc=pc: e.scalar_tensor_tensor(out=U[:, kt, :], in0=U[:, kt, :], scalar=din_, in1=pc,
                                                                          op0=ALU.mult, op1=ALU.add),
                      reads=[b_U, b_pc, b_tab], pwrites=[b_U])
            sc.op("dve", lambda e: e.scalar_tensor_tensor(out=Un, in0=Un, scalar=din_, in1=ps_dn, op0=ALU.mult, op1=ALU.add),
                  reads=[b_Un, b_ps_dn, b_tab], pwrites=[b_Un])
            sc.op("dve", lambda e: e.tensor_tensor(out=dm[:, 0:1], in0=ps_den, in1=thr, op=ALU.max),
                  reads=[b_ps_den, b_tab], writes=[b_dm])
            sc.op("dve", lambda e: e.scalar_tensor_tensor(out=dm[:, 0:1], in0=ps_den, scalar=-1.0, in1=dm[:, 0:1], op0=ALU.mult, op1=ALU.max),
                  reads=[b_ps_den, b_dm], pwrites=[b_dm])
            sc.op("dve", lambda e: e.reciprocal(out=dm[:, 1:2], in_=dm[:, 0:1]), reads=[b_dm], pwrites=[b_dm])
            if d == 0:
                hf, b_hf = Bf[("hf", slot)]
                sc.op("dve", lambda e: e.tensor_scalar(out=hf[:, ci, :], in0=ps_n, scalar1=dm[:, 1:2], scalar2=None, op0=ALU.mult),
                      reads=[b_ps_n, b_dm], pwrites=[b_hf])
            else:
                hf, b_hf = Bf[("hf", slot)]; so, b_so = Bf[("so", slot)]; sz, b_sz = Bf[("sz", slot)]
                t3, b_t3 = Bf[("t3", p2)]
                hs, b_hs = Bf[("hs", p2)]; ya, b_ya = Bf[("ya", p2)]; sq, b_sq = Bf[("sq", p2)]
                junk, b_junk = Bf["junk"]
                yT, b_yT = Bf[("yT", slot)]
                ps_tp, b_ps_tp = Bf["ps_tp"]
                sc.op("dve", lambda e: e.scalar_tensor_tensor(out=hs, in0=ps_n, scalar=dm[:, 1:2], in1=hf[:, ci, :], op0=ALU.mult, op1=ALU.add),
                      reads=[b_ps_n, b_dm, b_hf], writes=[b_hs])
                sc.op("act", lambda e: e.activation(out=junk, in_=hs, func=AF.Square), reads=[b_hs], writes=[b_junk])
                sc.op("dve", lambda e: e.reduce_sum(out=sq[:, 0:1], in_=junk, axis=AX.X), reads=[b_junk], writes=[b_sq])
                sc.op("act", lambda e: e.activation(out=sq[:, 1:2], in_=sq[:, 0:1], func=AF.Sqrt, bias=epsc[:, 0:1], scale=1.0 / 512.0),
                      reads=[b_sq, b_epsc], pwrites=[b_sq])
                sc.op("dve", lambda e: e.reciprocal(out=sq[:, 2:3], in_=sq[:, 1:2]), reads=[b_sq], pwrites=[b_sq])
                sc.op("dve", lambda e: e.scalar_tensor_tensor(out=t3, in0=hs, scalar=sq[:, 2:3], in1=so[:, ci, :], op0=ALU.mult, op1=ALU.mult),
                      reads=[b_hs, b_sq, b_so], writes=[b_t3])
                sc.op("dve", lambda e: e.tensor_tensor(out=ya, in0=t3, in1=sz[:, ci, :], op=ALU.mult),
                      reads=[b_t3, b_sz], writes=[b_ya])
                for f in range(4):
                    sc.op("pe", lambda e, f=f: e.transpose(out=ps_tp[:, f * L:(f + 1) * L], in_=ya[:, f * 128:(f + 1) * 128],
                                                           identity=identb[0:L, 0:L]),
                          reads=[b_ya, b_const], writes=[b_ps_tp] if f == 0 else (), pwrites=() if f == 0 else [b_ps_tp])
                sc.op("act", lambda e: e.activation(out=yT[:, :, cs], in_=ps_tp.rearrange("p (f t) -> p f t", f=4), func=AF.Copy),
                      reads=[b_ps_tp], pwrites=[b_yT])

        def store_group(hl, hg, d, g, slot):
            Bf = hl_bufs[hl]
            t0 = g * TG
            if d == 0:
                hf, b_hf = Bf[("hf", slot)]
                sc.op("sp", lambda e: e.dma_start(out=HF[t0:t0 + TG, hg * 512:(hg + 1) * 512].rearrange("(c p) f -> p c f", p=L), in_=hf),
                      reads=[b_hf], pwrites=[dbuf["HF"]], dma=b_hf)
            else:
                yT, b_yT = Bf[("yT", slot)]
                sc.op("sp", lambda e: e.dma_start(out=YAT[hg * 512:(hg + 1) * 512, t0:t0 + TG].rearrange("(f p) t -> p f t", p=128), in_=yT),
                      reads=[b_yT], pwrites=[dbuf["YAT"]], dma=b_yT)

        sc.newgen(dbuf["YAT"])
        for hp in range(HPC // 2):
            for d in range(2):
                if d == 0 and hp == 0:
                    sc.newgen(dbuf["HF"])
                order = list(range(NG)) if d == 0 else list(range(NG - 1, -1, -1))
                for hl in range(2):
                    U, b_U = hl_bufs[hl]["U"]; Un, b_Un = hl_bufs[hl]["Un"]
                    sc.op("dve", lambda e, U=U: e.memset(U, 0.0), writes=[b_U])
                    sc.op("dve", lambda e, Un=Un: e.memset(Un, 0.0), writes=[b_Un])
                    load_group(hl, 2 * hp + hl, d, order[0], 0)
                steps = []
                for si, g in enumerate(order):
                    cis = list(range(GC)) if d == 0 else list(range(GC - 1, -1, -1))
                    for k, ci in enumerate(cis):
                        steps.append((si, g, si % 2, ci, k == 0, k == GC - 1))

                def emit(st, stage):
                    si_, g_, slot_, ci_, _, _ = st
                    for hl in range(2):
                        chunk(hl, 2 * hp + hl, d, g_ * GC + ci_, slot_, ci_, stage)

                emit(steps[0], "A")
                for idx, st in enumerate(steps):
                    si, g, slot, ci, first, lastc = st
                    if first:
                        if si + 1 < NG:
                            for hl in range(2):
                                load_group(hl, 2 * hp + hl, d, order[si + 1], (si + 1) % 2)
                        for hl in range(2):
                            sc.newgen(hl_bufs[hl][("yT" if d == 1 else "hf", slot)][1])
                    emit(st, "B")
                    emit(st, "C")
                    if idx + 1 < len(steps):
                        emit(steps[idx + 1], "A")
                    emit(st, "D")
                    if lastc:
                        for hl in range(2):
                            store_group(hl, 2 * hp + hl, d, g, slot)

    def phase3(l):
        A.reset()
        CV = A.alloc([128, NCT, 12], F32); b_CV = Buf("CV")
        sc.op("sp", lambda e: e.dma_start(out=CV, in_=cvec[l].rearrange("(c p) k -> p c k", p=128)),
              reads=[IN], writes=[b_CV], dma=b_CV)
        C1 = A.alloc([128, NCT, 2], F32); b_C1 = Buf("C1")
        sc.op("act", lambda e: e.activation(out=C1, in_=CV[:, :, 9:11], func=AF.Exp, scale=-1.0), reads=[b_CV], writes=[b_C1])
        sc.op("act", lambda e: e.activation(out=C1, in_=C1, func=AF.Ln, bias=1.0, scale=1.0), reads=[b_C1], writes=[b_C1])
        sc.op("dve", lambda e: e.tensor_scalar(out=C1, in0=C1, scalar1=-8.0, scalar2=None, op0=ALU.mult), reads=[b_C1], writes=[b_C1])
        xb = [A.alloc([128, S], BF16) for _ in range(2)]; b_xb = [Buf("xb%d" % i) for i in range(2)]
        szb = [A.alloc([128, S], BF16) for _ in range(2)]; b_szb = [Buf("szb%d" % i) for i in range(2)]
        wr = [A.alloc([128, 4, 128], BF16) for _ in range(2)]; b_wr = [Buf("wr%d" % i) for i in range(2)]
        xc = A.alloc([128, S], F32); b_xc = Buf("xc")
        xcb = A.alloc([128, S], BF16); b_xcb = Buf("xcb")
        rr = A.alloc([128, S], F32); b_rr = Buf("rr")
        ig = A.alloc([128, S], F32); b_ig = Buf("ig")
        a2 = A.alloc([128, S], F32); b_a2 = Buf("a2")
        hh = [A.alloc([128, S], F32) for _ in range(2)]; b_hh = [Buf("hh%d" % i) for i in range(2)]
        ybt = [A.alloc([128, S], BF16) for _ in range(2)]; b_ybt = [Buf("ybt%d" % i) for i in range(2)]
        pc = [0]
        sc.newgen(dbuf["YBT"])

        def load_ct(ct):
            sl = ct % 2
            sc.op("sp", lambda e: e.dma_start(out=xb[sl], in_=XB[ct * 128:(ct + 1) * 128, :]), reads=[dbuf["XB"]], writes=[b_xb[sl]], dma=b_xb[sl])
            sc.op("sp", lambda e: e.dma_start(out=szb[sl], in_=ZB[ct * 128:(ct + 1) * 128, :]), reads=[dbuf["ZB"]], writes=[b_szb[sl]], dma=b_szb[sl])
            sc.op("pool", lambda e: e.dma_start(out=wr[sl], in_=wrg[l][:, ct].rearrange("g c d -> c g d")),
                  reads=[IN], writes=[b_wr[sl]], dma=b_wr[sl])

        load_ct(0)
        for ct in range(NCT):
            sl = ct % 2
            if ct + 1 < NCT:
                load_ct(ct + 1)
            x_ = xb[sl]; bx = b_xb[sl]
            cw = lambda k, ct=ct: CV[:, ct, k:k + 1]
            sc.op("dve", lambda e, x_=x_, cw=cw: e.tensor_scalar(out=xc, in0=x_, scalar1=cw(2), scalar2=cw(4), op0=ALU.mult, op1=ALU.add),
                  reads=[bx, b_CV], writes=[b_xc])
            sc.op("dve", lambda e, x_=x_, cw=cw: e.scalar_tensor_tensor(out=xc[:, 1:S], in0=x_[:, 0:S - 1], scalar=cw(1), in1=xc[:, 1:S],
                                                                        op0=ALU.mult, op1=ALU.add), reads=[bx, b_CV, b_xc], writes=[b_xc])
            sc.op("dve", lambda e, x_=x_, cw=cw: e.scalar_tensor_tensor(out=xc[:, 2:S], in0=x_[:, 0:S - 2], scalar=cw(0), in1=xc[:, 2:S],
                                                                        op0=ALU.mult, op1=ALU.add), reads=[bx, b_CV, b_xc], writes=[b_xc])
            sc.op("dve", lambda e, x_=x_, cw=cw: e.scalar_tensor_tensor(out=xc[:, 0:S - 1], in0=x_[:, 1:S], scalar=cw(3), in1=xc[:, 0:S - 1],
                                                                        op0=ALU.mult, op1=ALU.add), reads=[bx, b_CV, b_xc], writes=[b_xc])
            sc.op("act", lambda e: e.activation(out=xcb, in_=xc, func=AF.Copy), reads=[b_xc], writes=[b_xcb])
            for d in range(2):
                sc.newgen(b_rr); sc.newgen(b_ig)
                for tb in range(NB):
                    for gate in range(2):
                        bk = pc[0] % 8; pc[0] += 1
                        dst, b_dst = (rr, b_rr) if gate == 0 else (ig, b_ig)
                        sc.op("pe", lambda e, bk=bk, d=d, gate=gate, tb=tb, sl=sl: e.matmul(
                            banks[bk][:, :], lhsT=wr[sl][:, 2 * d + gate, :], rhs=xcb[:, tb * 512:(tb + 1) * 512], start=True, stop=True),
                            reads=[b_wr[sl], b_xcb], writes=[bankb[bk]])
                        sc.op("act", lambda e, bk=bk, d=d, gate=gate, tb=tb, dst=dst, cw=cw: e.activation(
                            out=dst[:, tb * 512:(tb + 1) * 512], in_=banks[bk][:, :], func=AF.Sigmoid, bias=cw(5 + 2 * d + gate)),
                            reads=[bankb[bk], b_CV], pwrites=[b_dst])
                c1 = C1[:, ct, d:d + 1]
                sc.op("dve", lambda e, c1=c1: e.tensor_scalar(out=rr, in0=rr, scalar1=c1, scalar2=None, op0=ALU.mult),
                      reads=[b_rr, b_C1], writes=[b_rr])
                sc.op("act", lambda e: e.activation(out=rr, in_=rr, func=AF.Exp), reads=[b_rr], writes=[b_rr])
                sc.op("act", lambda e: e.activation(out=a2, in_=rr, func=AF.Square), reads=[b_rr], writes=[b_a2])
                sc.op("act", lambda e, cw=cw: e.activation(out=a2, in_=a2, func=AF.Sqrt, bias=cw(11), scale=-1.0),
                      reads=[b_a2, b_CV], writes=[b_a2])
                sc.op("dve", lambda e: e.tensor_tensor(out=ig, in0=ig, in1=xc, op=ALU.mult), reads=[b_ig, b_xc], writes=[b_ig])
                sc.op("dve", lambda e: e.tensor_tensor(out=ig, in0=ig, in1=a2, op=ALU.mult), reads=[b_ig, b_a2], writes=[b_ig])
                rv = (lambda ap: ap) if d == 0 else (lambda ap: ap[:, ::-1])
                sc.op("dve", lambda e, d=d, rv=rv: e.tensor_tensor_scan(out=rv(hh[d]), data0=rv(rr), data1=rv(ig), initial=0.0,
                                                                      op0=ALU.mult, op1=ALU.add),
                      reads=[b_rr, b_ig], writes=[b_hh[d]])
            sc.op("dve", lambda e: e.tensor_tensor(out=hh[0], in0=hh[0], in1=hh[1], op=ALU.add), reads=[b_hh[0], b_hh[1]], writes=[b_hh[0]])
            sc.op("dve", lambda e, sl=sl: e.tensor_tensor(out=ybt[sl], in0=hh[0], in1=szb[sl], op=ALU.mult),
                  reads=[b_hh[0], b_szb[sl]], writes=[b_ybt[sl]])
            sc.op("sp", lambda e, sl=sl, ct=ct: e.dma_start(out=YBT[ct * 128:(ct + 1) * 128, :], in_=ybt[sl]),
                  reads=[b_ybt[sl]], pwrites=[dbuf["YBT"]], dma=b_ybt[sl])

    def phase4a(l):
        A.reset()
        NKA = VW // 128; NKB = CW // 128
        Wa = A.alloc([128, NKA, D], BF16); b_Wa = Buf("Wa")
        Wb = A.alloc([128, NKB, D], BF16); b_Wb = Buf("Wb")
        sc.newgen(b_Wa); sc.newgen(b_Wb)
        for k in range(NKA):
            sc.op("pool", lambda e, k=k: e.dma_start(out=Wa[:, k, :], in_=wa[l][k * 128:(k + 1) * 128, :]), reads=[IN], pwrites=[b_Wa], dma=b_Wa)
        for k in range(NKB):
            sc.op("pool", lambda e, k=k: e.dma_start(out=Wb[:, k, :], in_=wb[l][k * 128:(k + 1) * 128, :]), reads=[IN], pwrites=[b_Wb], dma=b_Wb)
        ya_b = [A.alloc([128, NKA, 512], BF16) for _ in range(2)]; b_ya_b = [Buf("ya_b%d" % i) for i in range(2)]
        yb_b = [A.alloc([128, NKB, 512], BF16) for _ in range(2)]; b_yb_b = [Buf("yb_b%d" % i) for i in range(2)]
        ga_b = [A.alloc([128, 16, 512], BF16) for _ in range(2)]; b_ga_b = [Buf("ga_b%d" % i) for i in range(2)]
        gb_b = [A.alloc([128, 16, 512], BF16) for _ in range(2)]; b_gb_b = [Buf("gb_b%d" % i) for i in range(2)]
        m1 = [A.alloc([128, 512], F32) for _ in range(2)]; b_m1 = [Buf("m1%d" % i) for i in range(2)]
        m2 = [A.alloc([128, 512], F32) for _ in range(2)]; b_m2 = [Buf("m2%d" % i) for i in range(2)]
        mt = [A.alloc([128, 512], BF16) for _ in range(2)]; b_mt = [Buf("mt%d" % i) for i in range(2)]
        sc.newgen(dbuf["MT"])
        u = [0]

        def load_tb(tb):
            sl = tb % 2
            ts = slice(tb * 512, (tb + 1) * 512)
            sc.op("sp", lambda e: e.dma_start(out=ya_b[sl], in_=YAT[:, ts].rearrange("(k p) t -> p k t", p=128)),
                  reads=[dbuf["YAT"]], writes=[b_ya_b[sl]], dma=b_ya_b[sl])
            sc.op("sp", lambda e: e.dma_start(out=yb_b[sl], in_=YBT[:, ts].rearrange("(k p) t -> p k t", p=128)),
                  reads=[dbuf["YBT"]], writes=[b_yb_b[sl]], dma=b_yb_b[sl])
            sc.op("sp", lambda e: e.dma_start(out=ga_b[sl], in_=GA[:, ts].rearrange("(k p) t -> p k t", p=128)),
                  reads=[dbuf["GA"]], writes=[b_ga_b[sl]], dma=b_ga_b[sl])
            sc.op("sp", lambda e: e.dma_start(out=gb_b[sl], in_=GB[:, ts].rearrange("(k p) t -> p k t", p=128)),
                  reads=[dbuf["GB"]], writes=[b_gb_b[sl]], dma=b_gb_b[sl])

        load_tb(0)
        for tb in range(NB):
            sl = tb % 2
            ts = slice(tb * 512, (tb + 1) * 512)
            if tb + 1 < NB:
                load_tb(tb + 1)
            for dt in range(16):
                i = u[0] % 2; u[0] += 1
                bka = (2 * u[0]) % 8; bkb = (2 * u[0] + 1) % 8
                for k in range(NKA):
                    sc.op("pe", lambda e, k=k, dt=dt, bka=bka, sl=sl: e.matmul(banks[bka][:, :], lhsT=Wa[:, k, dt * 128:(dt + 1) * 128], rhs=ya_b[sl][:, k, :],
                                                                             start=(k == 0), stop=(k == NKA - 1)),
                          reads=[b_Wa, b_ya_b[sl]], writes=[bankb[bka]] if k == 0 else (), pwrites=() if k == 0 else [bankb[bka]])
                for k in range(NKB):
                    sc.op("pe", lambda e, k=k, dt=dt, bkb=bkb, sl=sl: e.matmul(banks[bkb][:, :], lhsT=Wb[:, k, dt * 128:(dt + 1) * 128], rhs=yb_b[sl][:, k, :],
                                                                             start=(k == 0), stop=(k == NKB - 1)),
                          reads=[b_Wb, b_yb_b[sl]], writes=[bankb[bkb]] if k == 0 else (), pwrites=() if k == 0 else [bankb[bkb]])
                sc.op("dve", lambda e, i=i, dt=dt, bka=bka, sl=sl: e.tensor_tensor(out=m1[i], in0=banks[bka][:, :], in1=ga_b[sl][:, dt, :], op=ALU.mult),
                      reads=[bankb[bka], b_ga_b[sl]], writes=[b_m1[i]])
                sc.op("dve", lambda e, i=i, dt=dt, bkb=bkb, sl=sl: e.tensor_tensor(out=m2[i], in0=banks[bkb][:, :], in1=gb_b[sl][:, dt, :], op=ALU.mult),
                      reads=[bankb[bkb], b_gb_b[sl]], writes=[b_m2[i]])
                sc.op("dve", lambda e, i=i: e.tensor_tensor(out=mt[i], in0=m1[i], in1=m2[i], op=ALU.add),
                      reads=[b_m1[i], b_m2[i]], writes=[b_mt[i]])
                sc.op("sp", lambda e, i=i, dt=dt, ts=ts: e.dma_start(out=MT[dt * 128:(dt + 1) * 128, ts], in_=mt[i]),
                      reads=[b_mt[i]], pwrites=[dbuf["MT"]], dma=b_mt[i])

    def phase4b(l, xsrc, xsrc_buf, last):
        A.reset()
        Wo = A.alloc([128, 16, D], BF16); b_Wo = Buf("Wo")
        sc.newgen(b_Wo)
        for k in range(16):
            sc.op("pool", lambda e, k=k: e.dma_start(out=Wo[:, k, :], in_=wout[l][k * 128:(k + 1) * 128, :]), reads=[IN], pwrites=[b_Wo], dma=b_Wo)
        fgb = A.alloc([128, D], F32); b_fgb = Buf("fgb")
        if last:
            sc.op("sp", lambda e: e.dma_start(out=fgb, in_=finalg.partition_broadcast(128)), reads=[IN], writes=[b_fgb], dma=b_fgb)
        mtb = [A.alloc([128, 16, 512], BF16) for _ in range(2)]; b_mtb = [Buf("mtb%d" % i) for i in range(2)]
        xt = [A.alloc([128, D], F32) for _ in range(2)]; b_xt = [Buf("xt4%d" % i) for i in range(2)]
        xo = [A.alloc([128, D], F32) for _ in range(2)]; b_xo = [Buf("xo%d" % i) for i in range(2)]
        junk = A.alloc([128, D], BF16); b_junk = Buf("junk4")
        st = [A.alloc([128, 4], F32) for _ in range(2)]; b_st = [Buf("st4%d" % i) for i in range(2)]
        for i in range(2):
            sc.op("pool", lambda e, i=i: e.memset(st[i][:, 3:4], EPS), writes=[b_st[i]])
        SPLIT = cfg.split == 2
        b_cc = Buf("cc")
        fuse5 = SPLIT and last
        if SPLIT:
            dst, dname = PP, "PP"
            last = False
        else:
            dst, dname = (out, "OUT") if last else (X1, "X1")
        sc.newgen(dbuf[dname])
        if SPLIT:
            sc.newgen(dbuf["PS%d" % l])
        pc = [0]
        if fuse5:
            xt5 = [A.alloc([128, D], F32) for _ in range(2)]; b_xt5 = [Buf("xt5%d" % i) for i in range(2)]
            xo5 = [A.alloc([128, D], F32) for _ in range(2)]; b_xo5 = [Buf("xo5%d" % i) for i in range(2)]
            PS5 = PSL[l]
            b_PSblk = [Buf("PSblk%d" % i) for i in range(NB)]
            n_st1 = [0]
            sc.newgen(dbuf["OUT"])

            def f_st1(tt):
                i = tt % 2
                sc.op("pool", lambda e: e.dma_start(out=xt5[i], in_=PS5[tt * 128:(tt + 1) * 128, :]),
                      reads=[b_PSblk[tt // 4]], writes=[b_xt5[i]], dma=b_xt5[i])
                sc.op("act", lambda e: e.activation(out=junk, in_=xt5[i], func=AF.Square), reads=[b_xt5[i]], writes=[b_junk])
                sc.op("dve", lambda e: e.reduce_sum(out=st[i][:, 0:1], in_=junk, axis=AX.X), reads=[b_junk], writes=[b_st[i]])
                sc.op("act", lambda e: e.activation(out=st[i][:, 1:2], in_=st[i][:, 0:1], func=AF.Sqrt, bias=st[i][:, 3:4], scale=1.0 / D),
                      reads=[b_st[i]], pwrites=[b_st[i]])
                sc.op("dve", lambda e: e.reciprocal(out=st[i][:, 2:3], in_=st[i][:, 1:2]), reads=[b_st[i]], pwrites=[b_st[i]])

            def f_st2(tt):
                i = tt % 2
                sc.op("dve", lambda e: e.scalar_tensor_tensor(out=xo5[i], in0=xt5[i], scalar=st[i][:, 2:3], in1=fgb, op0=ALU.mult, op1=ALU.mult),
                      reads=[b_xt5[i], b_st[i], b_fgb], writes=[b_xo5[i]])
                sc.op("act", lambda e: e.dma_start(out=out[tt * 128:(tt + 1) * 128, :], in_=xo5[i]),
                      reads=[b_xo5[i]], pwrites=[dbuf["OUT"]], dma=b_xo5[i])

            def f_tile(p):
                while n_st1[0] <= min(p + 1, NT - 1):
                    f_st1(n_st1[0]); n_st1[0] += 1
                f_st2(p)

        def load_mt(tb):
            sl = tb % 2
            sc.op("act", lambda e: e.dma_start(out=mtb[sl], in_=MT[:, tb * 512:(tb + 1) * 512].rearrange("(k p) t -> p k t", p=128)),
                  reads=[dbuf["MT"]], writes=[b_mtb[sl]], dma=b_mtb[sl])

        load_mt(0)
        for tb in range(NB):
            sl = tb % 2
            if tb + 1 < NB:
                load_mt(tb + 1)
            for q in range(4):
                tt = tb * 4 + q
                i = tt % 2
                if tt == 0:
                    sc.op("sp", lambda e: e.dma_start(out=xt[0], in_=xsrc[0:128, :]),
                          reads=[xsrc_buf], writes=[b_xt[0]], dma=b_xt[0])
                if tt + 1 < NT:
                    sc.op("sp", lambda e, tt=tt: e.dma_start(out=xt[(tt + 1) % 2], in_=xsrc[(tt + 1) * 128:(tt + 2) * 128, :]),
                          reads=[xsrc_buf], writes=[b_xt[(tt + 1) % 2]], dma=b_xt[(tt + 1) % 2])
                sc.newgen(b_xo[i])
                for eb in range(4):
                    bk = pc[0] % 8; pc[0] += 1
                    for k in range(16):
                        sc.op("pe", lambda e, k=k, q=q, eb=eb, bk=bk, sl=sl: e.matmul(
                            banks[bk][:, :], lhsT=mtb[sl][:, k, q * 128:(q + 1) * 128], rhs=Wo[:, k, eb * 512:(eb + 1) * 512],
                            start=(k == 0), stop=(k == 15)),
                            reads=[b_mtb[sl], b_Wo], writes=[bankb[bk]] if k == 0 else (), pwrites=() if k == 0 else [bankb[bk]])
                    if SPLIT:
                        sc.op("dve", lambda e, eb=eb, bk=bk, i=i: e.scalar_tensor_tensor(
                            out=xo[i][:, eb * 512:(eb + 1) * 512], in0=xt[i][:, eb * 512:(eb + 1) * 512], scalar=flagt[:, 0:1],
                            in1=banks[bk][:, :], op0=ALU.mult, op1=ALU.add),
                            reads=[bankb[bk], b_xt[i], b_const], pwrites=[b_xo[i]])
                    else:
                        sc.op("dve", lambda e, eb=eb, bk=bk, i=i: e.tensor_tensor(out=xo[i][:, eb * 512:(eb + 1) * 512], in0=banks[bk][:, :],
                                                                                 in1=xt[i][:, eb * 512:(eb + 1) * 512], op=ALU.add),
                              reads=[bankb[bk], b_xt[i]], pwrites=[b_xo[i]])
                if last:
                    sc.op("act", lambda e, i=i: e.activation(out=junk, in_=xo[i], func=AF.Square), reads=[b_xo[i]], writes=[b_junk])
                    sc.op("dve", lambda e, i=i: e.reduce_sum(out=st[i][:, 0:1], in_=junk, axis=AX.X), reads=[b_junk], writes=[b_st[i]])
                    sc.op("act", lambda e, i=i: e.activation(out=st[i][:, 1:2], in_=st[i][:, 0:1], func=AF.Sqrt, bias=st[i][:, 3:4], scale=1.0 / D),
                          reads=[b_st[i]], pwrites=[b_st[i]])
                    sc.op("dve", lambda e, i=i: e.reciprocal(out=st[i][:, 2:3], in_=st[i][:, 1:2]), reads=[b_st[i]], pwrites=[b_st[i]])
                    sc.op("dve", lambda e, i=i: e.scalar_tensor_tensor(out=xo[i], in0=xo[i], scalar=st[i][:, 2:3], in1=fgb, op0=ALU.mult, op1=ALU.mult),
                          reads=[b_xo[i], b_st[i], b_fgb], writes=[b_xo[i]])
                sc.op("sp", lambda e, tt=tt, i=i: e.dma_start(out=dst[tt * 128:(tt + 1) * 128, :], in_=xo[i]),
                      reads=[b_xo[i]], pwrites=[dbuf[dname]], dma=b_xo[i])
                if fuse5 and tt >= 8:
                    f_tile(tt - 8)
            if SPLIT:
                PSl = PSL[l]
                sc.op("pool", lambda e, tb=tb, PSl=PSl: e.collective_compute(
                    "AllReduce", ALU.add, replica_groups=cfg.groups,
                    ins=[PP[tb * 512:(tb + 1) * 512, :].opt()], outs=[PSl[tb * 512:(tb + 1) * 512, :].opt()]),
                    reads=[dbuf["PP"]], pwrites=[dbuf["PS%d" % l]] + ([b_PSblk[tb]] if fuse5 else []), dma=b_cc, dinc=1)
        if fuse5:
            for p in range(max(0, NT - 8), NT):
                f_tile(p)

    def phase5(src, src_buf):
        A.reset()
        fgb = A.alloc([128, D], F32); b_fgb = Buf("fgb5")
        sc.op("sp", lambda e: e.dma_start(out=fgb, in_=finalg.partition_broadcast(128)), reads=[IN], writes=[b_fgb], dma=b_fgb)
        xt = [A.alloc([128, D], F32) for _ in range(2)]; b_xt = [Buf("xt5%d" % i) for i in range(2)]
        xo = [A.alloc([128, D], F32) for _ in range(2)]; b_xo = [Buf("xo5%d" % i) for i in range(2)]
        junk = A.alloc([128, D], BF16); b_junk = Buf("junk5")
        st = [A.alloc([128, 4], F32) for _ in range(2)]; b_st = [Buf("st5%d" % i) for i in range(2)]
        for i in range(2):
            sc.op("pool", lambda e, i=i: e.memset(st[i][:, 3:4], EPS), writes=[b_st[i]])
        sc.newgen(dbuf["OUT"])

        def st1(tt):
            i = tt % 2
            sc.op("sp", lambda e: e.dma_start(out=xt[i], in_=src[tt * 128:(tt + 1) * 128, :]),
                  reads=[src_buf], writes=[b_xt[i]], dma=b_xt[i])
            sc.op("act", lambda e: e.activation(out=junk, in_=xt[i], func=AF.Square), reads=[b_xt[i]], writes=[b_junk])
            sc.op("dve", lambda e: e.reduce_sum(out=st[i][:, 0:1], in_=junk, axis=AX.X), reads=[b_junk], writes=[b_st[i]])
            sc.op("act", lambda e: e.activation(out=st[i][:, 1:2], in_=st[i][:, 0:1], func=AF.Sqrt, bias=st[i][:, 3:4], scale=1.0 / D),
                  reads=[b_st[i]], pwrites=[b_st[i]])
            sc.op("dve", lambda e: e.reciprocal(out=st[i][:, 2:3], in_=st[i][:, 1:2]), reads=[b_st[i]], pwrites=[b_st[i]])

        def st2(tt):
            i = tt % 2
            sc.op("dve", lambda e: e.scalar_tensor_tensor(out=xo[i], in0=xt[i], scalar=st[i][:, 2:3], in1=fgb, op0=ALU.mult, op1=ALU.mult),
                  reads=[b_xt[i], b_st[i], b_fgb], writes=[b_xo[i]])
            sc.op("sp", lambda e: e.dma_start(out=out[tt * 128:(tt + 1) * 128, :], in_=xo[i]),
                  reads=[b_xo[i]], pwrites=[dbuf["OUT"]], dma=b_xo[i])

        st1(0)
        for tt in range(NT):
            if tt + 1 < NT:
                st1(tt + 1)
            st2(tt)

    xsrc, xsrc_buf = x_in, IN
    for l in range(DEPTH):
        layer(l, xsrc, xsrc_buf, l == DEPTH - 1)
        if cfg.stop is not None:
            break
        if cfg.split == 2:
            xsrc, xsrc_buf = PSL[l], dbuf["PS%d" % l]
        else:
            xsrc, xsrc_buf = X1, dbuf["X1"]
    if cfg.split == 2 and cfg.stop is None:
        if "PPd" in cfg.debug:
            dump("PP", PP, dbuf["PP"], [S, D])
            dump("PS", xsrc, xsrc_buf, [S, D])
        pass
    sc.barrier()

    with nc.Block() as block:
        @block.tensor
        def _(e):
            sc.replay(e, "pe")

        @block.scalar
        def _(e):
            sc.replay(e, "act")

        @block.vector
        def _(e):
            sc.replay(e, "dve")

        @block.gpsimd
        def _(e):
            sc.replay(e, "pool")

        @block.sync
        def _(e):
            sc.replay(e, "sp")
    return nc


def make_consts(HPC):
    ident = np.eye(128, dtype=np.float32)
    mask = np.zeros((L, 2, L), np.float32)
    s_idx = np.arange(L)[:, None]
    j_idx = np.arange(L)[None, :]
    mask[:, 0, :] = (s_idx <= j_idx)
    mask[:, 1, :] = (s_idx >= j_idx)
    sel = np.zeros((HPC, HPC, 128), np.float32)
    for h in range(HPC):
        sel[h, h, :] = 1.0
    return ident, mask, sel


def prep_core(inp, b, heads, cts, S, depth):
    HPC = len(heads)
    NCT = len(cts)
    qc = np.concatenate([np.arange(h * 256, (h + 1) * 256) for h in heads])
    vc = np.concatenate([np.arange(h * 512, (h + 1) * 512) for h in heads])
    cc = np.concatenate([np.arange(c * 128, (c + 1) * 128) for c in cts])
    O_Q, O_K, O_V, O_O, O_ZA, O_G = 0, 1024, 2048, 4096, 6144, 8192
    O_XB, O_ZB, O_GA, O_GB = 8208, 10256, 12304, 14352
    fm_cols = np.concatenate([O_Q + qc, O_K + qc, O_XB + cc, O_ZB + cc, O_GA + np.arange(D), O_GB + np.arange(D)])
    tm_cols = np.concatenate([O_K + qc, O_V + vc, O_O + vc, O_ZA + vc])
    hs = np.array(heads)
    g_cols = np.concatenate([O_G + hs, O_G + 8 + hs, O_G + 4 + hs, O_G + 12 + hs])
    b_idx = np.concatenate([hs, 8 + hs, 4 + hs, 12 + hs])
    w_in = inp["w_in"]
    d = {}
    d["x"] = np.ascontiguousarray(inp["x"][b, :S])
    d["wfm"] = np.ascontiguousarray(w_in[:depth][:, :, fm_cols])
    d["wtm"] = np.ascontiguousarray(w_in[:depth][:, :, tm_cols])
    wgc = w_in[:depth][:, :, g_cols]
    d["wg"] = np.ascontiguousarray(wgc.reshape(depth, 16, 128, -1).transpose(0, 2, 1, 3).reshape(depth, 128, -1))
    d["bg"] = np.ascontiguousarray(inp["b_if"][:depth][:, b_idx][:, :, None])
    d["normg"] = np.ascontiguousarray(inp["norm_g"][:depth])
    d["headg"] = np.ascontiguousarray(inp["head_g"][:depth][:, vc])
    cv = np.zeros((depth, NCT * 128, 12), np.float32)
    cv[:, :, 0:4] = np.transpose(inp["conv_w"][:depth][:, :, cc], (0, 2, 1))
    cv[:, :, 4] = inp["conv_b"][:depth][:, cc]
    brg = inp["b_rg"][:depth]
    cv[:, :, 5] = brg[:, 0, 0][:, cc]
    cv[:, :, 6] = brg[:, 0, 1][:, cc]
    cv[:, :, 7] = brg[:, 1, 0][:, cc]
    cv[:, :, 8] = brg[:, 1, 1][:, cc]
    cv[:, :, 9] = inp["lru_lambda"][:depth][:, 0][:, cc]
    cv[:, :, 10] = inp["lru_lambda"][:depth][:, 1][:, cc]
    cv[:, :, 11] = 1.0
    d["cvec"] = cv
    wr = inp["w_rg"][:depth]
    d["wrg"] = np.ascontiguousarray(wr[:, :, :, list(cts)].reshape(depth, 4, NCT, 128, 128))
    d["wa"] = np.ascontiguousarray(inp["w_branch_a"][:depth][:, vc, :])
    d["wb"] = np.ascontiguousarray(inp["w_branch_b"][:depth][:, cc, :])
    d["wout"] = np.ascontiguousarray(inp["w_out"][:depth])
    d["finalg"] = np.ascontiguousarray(inp["final_g"])
    ident, mask, sel = make_consts(HPC)
    d["c_ident"], d["c_mask"], d["c_sel"] = ident, mask, sel
    d["c_flag"] = np.ones((128, 1), np.float32)
    return d


_PROG = {}


def kernel(**inputs):
    inp = {k: np.asarray(v) for k, v in inputs.items()}
    B, S, _ = inp["x"].shape
    cfg = Cfg(S=S, HPC=2, NCT=8, DEPTH=2, split=2, groups=[[0, 1], [2, 3], [4, 5], [6, 7]])
    key = (S,)
    if key not in _PROG:
        _PROG[key] = build_program(cfg)
    nc = _PROG[key]
    halves = [prep_core(inp, 0, [2 * hh, 2 * hh + 1], list(range(8 * hh, 8 * hh + 8)), S, 2) for hh in range(2)]
    in_maps = []
    for core in range(2 * B):
        b, hh = core // 2, core % 2
        m = dict(halves[hh])
        m["x"] = np.ascontiguousarray(inp["x"][b, :S])
        m["c_flag"] = np.full((128, 1), 1.0 if hh == 0 else 0.0, np.float32)
        in_maps.append(m)
    res = run_bass_kernel_spmd(nc, in_maps, core_ids=list(range(2 * B)))
    outs = [np.asarray(res.results[2 * b]["out"]) for b in range(B)]
    return np.stack(outs, 0).astype(np.float32)
```
